# Optimizing a Trainium2 kernel written in Bass

```python
import math
import jax
import jax.numpy as jnp
from jax import lax
import numpy as np

D_MODEL = 2048
BATCH = 2
SEQ = 8192
DEPTH = 2

NUM_MIXERS = 2
EPS = 1e-6

ATT_HEAD_DIM = 64
ATT_Q_HEADS = D_MODEL // ATT_HEAD_DIM
ATT_KV_HEADS = ATT_Q_HEADS // 8
ATT_GROUP = ATT_Q_HEADS // ATT_KV_HEADS
WINDOW = 128
ATT_BLOCK = WINDOW
ROPE_THETA = 10000.0
QKV_WIDTH = (ATT_Q_HEADS + 2 * ATT_KV_HEADS) * ATT_HEAD_DIM

SSM_D_INNER = 2 * D_MODEL
SSM_HEAD_DIM = 64
SSM_HEADS = SSM_D_INNER // SSM_HEAD_DIM
SSM_GROUPS = 8
SSM_HEADS_PER_GROUP = SSM_HEADS // SSM_GROUPS
SSM_STATE = 128
SSM_CONV = 4
SSM_CHUNK = 256
SSM_CONV_DIM = SSM_D_INNER + 2 * SSM_GROUPS * SSM_STATE
SSM_IN_WIDTH = SSM_D_INNER + SSM_CONV_DIM + SSM_HEADS
SSM_NORM_GROUP = SSM_D_INNER // SSM_GROUPS

D_FF = -(-(8 * D_MODEL) // (3 * 256)) * 256

N_ATTN_LAYERS = (DEPTH + NUM_MIXERS - 1) // NUM_MIXERS
N_SSM_LAYERS = DEPTH // NUM_MIXERS

kernel_name = 'hybrid_swa_sink_mamba2_swiglu'


def rms_norm(x, gain):
    xf = x.astype(jnp.float32)
    y = xf * lax.rsqrt(jnp.mean(xf * xf, axis=-1, keepdims=True) + EPS)
    return (y * gain.astype(jnp.float32)).astype(x.dtype)


def rope_tables(positions):
    inv_freq = ROPE_THETA ** (-jnp.arange(0, ATT_HEAD_DIM, 2, dtype=jnp.float32) / ATT_HEAD_DIM)
    ang = positions.astype(jnp.float32)[..., None] * inv_freq
    return jnp.cos(ang)[:, :, None, :], jnp.sin(ang)[:, :, None, :]


def apply_rope(t, cos, sin):
    tf = t.astype(jnp.float32)
    t1, t2 = jnp.split(tf, 2, axis=-1)
    return jnp.concatenate([t1 * cos - t2 * sin, t2 * cos + t1 * sin], axis=-1).astype(t.dtype)


def sliding_window_attention(h, positions, w_qkv, q_norm, k_norm, sinks, w_o):
    b, s, _ = h.shape
    qkv = h @ w_qkv
    q, k, v = jnp.split(qkv, [ATT_Q_HEADS * ATT_HEAD_DIM, (ATT_Q_HEADS + ATT_KV_HEADS) * ATT_HEAD_DIM], axis=-1)
    q = q.reshape(b, s, ATT_Q_HEADS, ATT_HEAD_DIM)
    k = k.reshape(b, s, ATT_KV_HEADS, ATT_HEAD_DIM)
    v = v.reshape(b, s, ATT_KV_HEADS, ATT_HEAD_DIM)
    q = rms_norm(q, q_norm)
    k = rms_norm(k, k_norm)
    cos, sin = rope_tables(positions)
    q = apply_rope(q, cos, sin)
    k = apply_rope(k, cos, sin)

    nb = s // ATT_BLOCK
    qb = q.reshape(b, nb, ATT_BLOCK, ATT_KV_HEADS, ATT_GROUP, ATT_HEAD_DIM)
    kb = k.reshape(b, nb, ATT_BLOCK, ATT_KV_HEADS, ATT_HEAD_DIM)
    vb = v.reshape(b, nb, ATT_BLOCK, ATT_KV_HEADS, ATT_HEAD_DIM)
    shift = ((0, 0), (1, 0), (0, 0), (0, 0), (0, 0))
    kw = jnp.concatenate([jnp.pad(kb, shift)[:, :-1], kb], axis=2)
    vw = jnp.concatenate([jnp.pad(vb, shift)[:, :-1], vb], axis=2)

    scale = ATT_HEAD_DIM ** -0.5
    scores = jnp.einsum('bnqhgd,bnkhd->bnhgqk', qb, kw).astype(jnp.float32) * scale
    qi = jnp.arange(ATT_BLOCK)[:, None]
    kj = jnp.arange(2 * ATT_BLOCK)[None, :]
    band = (kj > qi) & (kj <= qi + ATT_BLOCK)
    not_before_start = (jnp.arange(nb) > 0)[:, None, None] | (kj >= ATT_BLOCK)[None]
    valid = band[None] & not_before_start
    scores = jnp.where(valid[None, :, None, None], scores, -jnp.inf)

    sink = sinks.astype(jnp.float32).reshape(ATT_KV_HEADS, ATT_GROUP)[None, None, :, :, None, None]
    m = jnp.maximum(jnp.max(scores, axis=-1, keepdims=True), sink)
    p = jnp.exp(scores - m)
    probs = p / (jnp.sum(p, axis=-1, keepdims=True) + jnp.exp(sink - m))
    out = jnp.einsum('bnhgqk,bnkhd->bnqhgd', probs.astype(vw.dtype), vw)
    return out.reshape(b, s, ATT_Q_HEADS * ATT_HEAD_DIM) @ w_o


def causal_depthwise_conv(u, w, bias):
    out = lax.conv_general_dilated(u, w[:, None, :].astype(u.dtype), window_strides=(1,),
                                   padding=[(SSM_CONV - 1, 0)],
                                   dimension_numbers=('NWC', 'WIO', 'NWC'),
                                   feature_group_count=u.shape[-1])
    return out + bias.astype(u.dtype)


def mamba2_ssd(h, w_in, conv_w, conv_b, dt_bias, a_log, d_skip, norm_g, w_out):
    b, s, _ = h.shape
    G, R, P, N, L = SSM_GROUPS, SSM_HEADS_PER_GROUP, SSM_HEAD_DIM, SSM_STATE, SSM_CHUNK
    zxbcdt = h @ w_in
    z = zxbcdt[..., :SSM_D_INNER]
    xbc = zxbcdt[..., SSM_D_INNER:SSM_D_INNER + SSM_CONV_DIM]
    dt = zxbcdt[..., SSM_D_INNER + SSM_CONV_DIM:]
    xbc = jax.nn.silu(causal_depthwise_conv(xbc, conv_w, conv_b))
    xs, bm, cm = jnp.split(xbc, [SSM_D_INNER, SSM_D_INNER + G * N], axis=-1)

    xs = xs.astype(jnp.float32).reshape(b, s, G, R, P)
    bm = bm.astype(jnp.float32).reshape(b, s, G, N)
    cm = cm.astype(jnp.float32).reshape(b, s, G, N)
    dt = jax.nn.softplus(dt.astype(jnp.float32) + dt_bias.astype(jnp.float32)).reshape(b, s, G, R)
    a = -jnp.exp(a_log.astype(jnp.float32)).reshape(G, R)

    pad = (-s) % L
    nc = (s + pad) // L

    def to_chunks(t):
        t = jnp.pad(t, [(0, 0), (0, pad)] + [(0, 0)] * (t.ndim - 2))
        return jnp.moveaxis(t.reshape((b, nc, L) + t.shape[2:]), 1, 0)

    causal = jnp.tril(jnp.ones((L, L), dtype=bool))[None, :, :, None, None]

    def chunk_step(state, inp):
        xc, dtc, bc, cc = inp
        acum = jnp.cumsum(dtc * a, axis=1)
        seg = acum[:, :, None] - acum[:, None, :]
        decay = jnp.exp(jnp.where(causal, seg, -jnp.inf))
        xdt = xc * dtc[..., None]
        cb = jnp.einsum('blgn,bsgn->blsg', cc, bc)
        y_diag = jnp.einsum('blsgr,bsgrp->blgrp', cb[..., None] * decay, xdt)
        y_off = jnp.einsum('blgn,bgrpn->blgrp', cc, state) * jnp.exp(acum)[..., None]
        to_end = jnp.exp(acum[:, -1:] - acum)
        new_state = (state * jnp.exp(acum[:, -1])[..., None, None]
                     + jnp.einsum('bsgn,bsgr,bsgrp->bgrpn', bc, to_end, xdt))
        return new_state, y_diag + y_off

    init = jnp.zeros((b, G, R, P, N), jnp.float32)
    _, y = lax.scan(chunk_step, init, (to_chunks(xs), to_chunks(dt), to_chunks(bm), to_chunks(cm)))
    y = jnp.moveaxis(y, 0, 1).reshape(b, nc * L, G, R, P)[:, :s]
    y = y + d_skip.astype(jnp.float32).reshape(G, R)[:, :, None] * xs
    y = y.reshape(b, s, SSM_D_INNER) * jax.nn.silu(z.astype(jnp.float32))
    y = y.reshape(b, s, G, SSM_NORM_GROUP)
    y = y * lax.rsqrt(jnp.mean(y * y, axis=-1, keepdims=True) + EPS)
    y = y.reshape(b, s, SSM_D_INNER) * norm_g.astype(jnp.float32)
    return y.astype(h.dtype) @ w_out


def swiglu(h, w_gate, w_up, w_down):
    return (jax.nn.silu(h @ w_gate) * (h @ w_up)) @ w_down


def setup_inputs(seed: int = 0) -> dict:
    key = jax.random.key(seed)
    ks = jax.random.split(key, 24)
    f32 = jnp.float32
    resid = (2 * DEPTH) ** -0.5

    def nrm(k, shape, fan_in, scale=1.0):
        return jax.random.normal(k, shape, f32) * (scale * fan_in ** -0.5)

    def gain(k, shape):
        return 1.0 + 0.02 * jax.random.normal(k, shape, f32)

    x = jax.random.normal(ks[0], (BATCH, SEQ, D_MODEL), f32)
    start = jax.random.randint(ks[1], (BATCH, 1), 0, 4096)
    positions = (start + jnp.arange(SEQ)[None, :]).astype(jnp.int32)

    dt0 = jnp.exp(jax.random.uniform(ks[14], (N_SSM_LAYERS, SSM_HEADS), f32)
                  * (math.log(0.1) - math.log(0.001)) + math.log(0.001))
    return {
        'x': x,
        'positions': positions,
        'mixer_norm': gain(ks[2], (DEPTH, D_MODEL)),
        'ffn_norm': gain(ks[3], (DEPTH, D_MODEL)),
        'attn_w_qkv': nrm(ks[4], (N_ATTN_LAYERS, D_MODEL, QKV_WIDTH), D_MODEL),
        'attn_q_norm': gain(ks[5], (N_ATTN_LAYERS, ATT_HEAD_DIM)),
        'attn_k_norm': gain(ks[6], (N_ATTN_LAYERS, ATT_HEAD_DIM)),
        'attn_sinks': 0.5 * jax.random.normal(ks[7], (N_ATTN_LAYERS, ATT_Q_HEADS), f32),
        'attn_w_o': nrm(ks[8], (N_ATTN_LAYERS, ATT_Q_HEADS * ATT_HEAD_DIM, D_MODEL), ATT_Q_HEADS * ATT_HEAD_DIM, resid),
        'ssm_w_in': nrm(ks[9], (N_SSM_LAYERS, D_MODEL, SSM_IN_WIDTH), D_MODEL),
        'ssm_conv_w': nrm(ks[10], (N_SSM_LAYERS, SSM_CONV, SSM_CONV_DIM), SSM_CONV),
        'ssm_conv_b': 0.02 * jax.random.normal(ks[11], (N_SSM_LAYERS, SSM_CONV_DIM), f32),
        'ssm_dt_bias': dt0 + jnp.log(-jnp.expm1(-dt0)),
        'ssm_a_log': jnp.log(jax.random.uniform(ks[12], (N_SSM_LAYERS, SSM_HEADS), f32, 1.0, 16.0)),
        'ssm_d': gain(ks[13], (N_SSM_LAYERS, SSM_HEADS)),
        'ssm_norm': gain(ks[15], (N_SSM_LAYERS, SSM_D_INNER)),
        'ssm_w_out': nrm(ks[16], (N_SSM_LAYERS, SSM_D_INNER, D_MODEL), SSM_D_INNER, resid),
        'ffn_w_gate': nrm(ks[17], (DEPTH, D_MODEL, D_FF), D_MODEL),
        'ffn_w_up': nrm(ks[18], (DEPTH, D_MODEL, D_FF), D_MODEL),
        'ffn_w_down': nrm(ks[19], (DEPTH, D_FF, D_MODEL), D_FF, resid),
    }


def reference(x, positions, mixer_norm, ffn_norm, attn_w_qkv, attn_q_norm, attn_k_norm, attn_sinks,
              attn_w_o, ssm_w_in, ssm_conv_w, ssm_conv_b, ssm_dt_bias, ssm_a_log, ssm_d, ssm_norm,
              ssm_w_out, ffn_w_gate, ffn_w_up, ffn_w_down):
    for i in range(DEPTH):
        h = rms_norm(x, mixer_norm[i])
        j = i // NUM_MIXERS
        if i % NUM_MIXERS == 0:
            x = x + sliding_window_attention(h, positions, attn_w_qkv[j], attn_q_norm[j], attn_k_norm[j],
                                             attn_sinks[j], attn_w_o[j])
        else:
            x = x + mamba2_ssd(h, ssm_w_in[j], ssm_conv_w[j], ssm_conv_b[j], ssm_dt_bias[j], ssm_a_log[j],
                               ssm_d[j], ssm_norm[j], ssm_w_out[j])
        x = x + swiglu(rms_norm(x, ffn_norm[i]), ffn_w_gate[i], ffn_w_up[i], ffn_w_down[i])
    return x
```

```python
import contextlib
import math
import numpy as np
import concourse.bass as bass
import concourse.mybir as mybir
from concourse.bass_utils import run_bass_kernel_spmd

F32 = mybir.dt.float32
BF16 = mybir.dt.bfloat16
I32 = mybir.dt.int32
AF = mybir.ActivationFunctionType
ALU = mybir.AluOpType
AX = mybir.AxisListType

D = 2048
SEQ = 8192
NCORES = 8
TOK = 2048
MT = 512
NMT = TOK // MT
KT = D // 128
DFF = 5632
FT = DFF // 128
QKVW = 2560
DIN = 4096
INW = 10304
EPS = 1e-6
TWO_PI = 2.0 * math.pi


class Sched:
    ENG = ("pe", "act", "dve", "pool", "sp")

    def __init__(self, nc, n_dma_sems=8):
        self.nc = nc
        self.ops = []
        self.last_writer = {}
        self.readers = {}
        self.n_dma_sems = n_dma_sems

    def op(self, eng, fn, reads=(), writes=(), dma=False):
        i = len(self.ops)
        deps = {}
        for k in reads:
            w = self.last_writer.get(k)
            if w is not None:
                deps[w] = True
        for k in writes:
            w = self.last_writer.get(k)
            if w is not None:
                deps.setdefault(w, False)
            for r in self.readers.get(k, ()):
                deps.setdefault(r, False)
        for k in reads:
            self.readers.setdefault(k, []).append(i)
        for k in writes:
            self.last_writer[k] = i
            self.readers[k] = []
        deps.pop(i, None)
        self.ops.append(dict(eng=eng, fn=fn, deps=deps, dma=dma))
        return i

    def emit(self, final_wait_ops=()):
        nc = self.nc
        ops = self.ops
        has_dep = [False] * len(ops)
        for o in ops:
            for d in o["deps"]:
                has_dep[d] = True
        for d in final_wait_ops:
            has_dep[d] = True
        with contextlib.ExitStack() as stack:
            esem = {e: stack.enter_context(nc.semaphore("s_" + e)) for e in ("pe", "act", "dve", "pool")}
            dsem = {e: [stack.enter_context(nc.semaphore("d_%s%d" % (e, j))) for j in range(self.n_dma_sems)]
                    for e in ("sp", "act", "pool")}
            ecount = {e: 0 for e in esem}
            dcount = {e: [0] * self.n_dma_sems for e in dsem}
            dnext = {e: 0 for e in dsem}
            dlast = {e: [None] * self.n_dma_sems for e in dsem}
            done = [None] * len(ops)
            for i, o in enumerate(ops):
                e = o["eng"]
                if o["dma"]:
                    j = dnext[e] % self.n_dma_sems
                    dnext[e] += 1
                    if dlast[e][j] is not None:
                        o["deps"][dlast[e][j]] = True
                    dlast[e][j] = i
                    dcount[e][j] += 16
                    done[i] = (dsem[e][j], dcount[e][j], "dma")
                elif has_dep[i]:
                    ecount[e] += 1
                    done[i] = (esem[e], ecount[e], e)
            per_eng = {e: [] for e in self.ENG}
            for i, o in enumerate(ops):
                per_eng[o["eng"]].append(i)
            block = stack.enter_context(nc.Block())

            def make_body(e):
                def body(eng):
                    waited = {}
                    for i in per_eng[e]:
                        o = ops[i]
                        for d in sorted(o["deps"]):
                            sem, val, src = done[d]
                            if src == e and e == "pe":
                                continue
                            key = id(sem)
                            if waited.get(key, 0) >= val:
                                continue
                            waited[key] = val
                            eng.wait_ge(sem, val)
                        ins = o["fn"](eng)
                        if done[i] is not None:
                            sem, val, src = done[i]
                            ins.then_inc(sem, 16 if o["dma"] else 1)
                    if e == "sp":
                        for d in final_wait_ops:
                            sem, val, src = done[d]
                            eng.wait_ge(sem, val)
                return body

            block.tensor(make_body("pe"))
            block.scalar(make_body("act"))
            block.vector(make_body("dve"))
            block.gpsimd(make_body("pool"))
            block.sync(make_body("sp"))


class Builder:
    def __init__(self, nstage=2):
        self.nstage = nstage
        self.nc = bass.Bass("TRN2", target_bir_lowering=False)
        self.S = Sched(self.nc)
        nc = self.nc
        self.stage = [nc.alloc_sbuf_tensor("wst%d" % i, [128, 4096], F32).ap() for i in range(nstage)]
        self.ring = [nc.alloc_sbuf_tensor("wrg%d" % i, [128, 4096], BF16).ap() for i in range(3)]
        self.n_w = 0
        self.ps = [nc.alloc_psum_tensor("ps%d" % i, [128, 512], F32).ap() for i in range(6)]
        self.pt = [nc.alloc_psum_tensor("pt%d" % i, [128, 1024], BF16).ap() for i in range(2)]
        self.n_pt = 0
        self.n_ps = 0
        self.nrot = 6
        self.xres = self.sb("xres", [128, 4, D], F32)
        self.hT = self.sb("hT", [128, KT, MT], BF16)
        self.arena = self.sb("arena", [128, 44 * 512], BF16)
        self.xn = [self.sb("xn%d" % i, [128, D], BF16) for i in range(2)]
        self.junk = self.sb("junk", [128, D], BF16)
        self.stat = self.sb("stat", [128, 8], F32)
        self.sg = [self.sb("sg%d" % i, [128, MT], BF16) for i in range(2)]
        self.ident = self.sb("ident", [128, 128], BF16)
        self.ones = self.sb("ones", [128, 128], BF16)
        self.epsb = self.sb("epsb", [128, 1], F32)
        self.oneb = self.sb("oneb", [128, 1], F32)
        self.I("pool", lambda e: e.memset(self.ones, 1.0), w=["ones"])
        self.I("pool", lambda e: e.memset(self.epsb, EPS), w=["epsb"])
        self.I("pool", lambda e: e.memset(self.oneb, 1.0), w=["oneb"])
        self.I("pool", lambda e: e.affine_select(out=self.ident, in_=self.ones, pattern=[[-1, 128]],
                                                 compare_op=ALU.is_equal, fill=0.0, base=0, channel_multiplier=1),
               r=["ones"], w=["ident"])
        self.I("dve", lambda e: e.memset(self.stat[:, 4:5], 0.0), w=["arena_own"])

    def I(self, eng, fn, r=(), w=(), dma=False):
        return self.S.op(eng, fn, r, w, dma)

    def sb(self, name, shape, dt):
        return self.nc.alloc_sbuf_tensor("sb_" + name, shape, dt).ap()

    def dram(self, name, shape, dt, kind="Internal"):
        return self.nc.dram_tensor(name, shape, dt, kind=kind).ap()

    def wload(self, W, r0, KC, c0, NC):
        n = self.n_w
        self.n_w += 1
        st, rg = self.stage[n % self.nstage], self.ring[n % 3]
        sk, rk = ("wst", n % self.nstage), ("wrg", n % 3)
        src = W[r0:r0 + KC * 128, c0:c0 + NC].rearrange("(k p) n -> p k n", p=128)
        stv = st[:, 0:KC * NC].rearrange("p (k n) -> p k n", k=KC)
        self.I("sp", lambda e: e.dma_start(out=stv, in_=src), w=[sk], dma=True)
        self.I("pool", lambda e: e.tensor_copy(out=rg[:, 0:KC * NC], in_=st[:, 0:KC * NC]), r=[sk], w=[rk])
        return rg[:, 0:KC * NC].rearrange("p (k n) -> p k n", k=KC), rk

    def ptbank(self):
        i = self.n_pt % 2
        self.n_pt += 1
        return self.pt[i], ("pt", i)

    def psbank(self):
        i = self.n_ps % self.nrot
        self.n_ps += 1
        return i

    def load_cols(self, name, src, ncol, dt=F32):
        t = self.sb(name, [128, ncol], dt)
        self.I("sp", lambda e: e.dma_start(out=t, in_=src), w=[name], dma=True)
        return t

    def arena_barrier(self):
        self.I("dve", lambda e: e.memset(self.stat[:, 4:5], 0.0), w=["arena_own"])

    def norm_T(self, gcol, gkey, ntt=4, src=None, srckeys=None, dst=None, dstkey="hT", xr=()):
        xr = list(xr)
        src = self.xres if src is None else src
        dst = self.hT if dst is None else dst
        for tt in range(ntt):
            xk = ("xres", tt) if srckeys is None else srckeys[tt]
            xin = src[:, tt, :]
            xn, xnk = self.xn[tt % 2], ("xn", tt % 2)
            ss = self.stat[:, (tt % 2) * 2:(tt % 2) * 2 + 1]
            rs = self.stat[:, (tt % 2) * 2 + 1:(tt % 2) * 2 + 2]
            ssk, rsk = ("nss", tt % 2), ("nrs", tt % 2)
            self.I("act", lambda e, xin=xin, ss=ss: e.activation(out=self.junk, in_=xin, func=AF.Square, accum_out=ss),
                   r=[xk] + xr, w=["junk", ssk])
            self.I("act", lambda e, ss=ss, rs=rs: e.activation(out=rs, in_=ss, func=AF.Sqrt, bias=self.epsb, scale=1.0 / D),
                   r=[ssk, "epsb"], w=[rsk])
            self.I("dve", lambda e, rs=rs: e.reciprocal(out=rs, in_=rs), r=[rsk], w=[rsk])
            self.I("act", lambda e, xin=xin, xn=xn, rs=rs: e.activation(out=xn, in_=xin, func=AF.Copy, scale=rs),
                   r=[xk, rsk] + xr, w=[xnk])
            for half in range(2):
                pt, ptk = self.ptbank()

                def tr(e, pt=pt, xn=xn, half=half):
                    r = None
                    for j in range(8):
                        k = half * 8 + j
                        r = e.transpose(out=pt[:, j * 128:(j + 1) * 128], in_=xn[:, k * 128:(k + 1) * 128], identity=self.ident)
                    return r
                self.I("pe", tr, r=[xnk, "ident"], w=[ptk])
                self.I("dve", lambda e, pt=pt, half=half, tt=tt: e.tensor_tensor(
                    out=dst[:, half * 8:(half + 1) * 8, tt * 128:(tt + 1) * 128],
                    in0=pt.rearrange("p (k t) -> p k t", k=8),
                    in1=gcol[:, half * 8:(half + 1) * 8].unsqueeze(2).to_broadcast([128, 8, 128]), op=ALU.mult),
                    r=[ptk, gkey] + xr, w=[(dstkey, tt)])

    def proj_out(self, actT, actkeys, nk, W, kc=8):
        nch = (nk + kc - 1) // kc
        for cb in range(4):
            banks = [self.psbank() for tt in range(4)]
            for ci in range(nch):
                k0 = ci * kc
                kk_n = min(kc, nk - k0)
                wv, wk = self.wload(W, k0 * 128, kk_n, cb * 512, 512)
                for tt in range(4):
                    def mm(e, wv=wv, tt=tt, k0=k0, kk_n=kk_n, b=banks[tt]):
                        r = None
                        for kk in range(kk_n):
                            k = k0 + kk
                            r = e.matmul(self.ps[b], lhsT=actT[:, k, tt * 128:(tt + 1) * 128], rhs=wv[:, kk, :],
                                         start=(k == 0), stop=(k == nk - 1))
                        return r
                    self.I("pe", mm, r=[wk] + list(actkeys), w=[("ps", banks[tt])])
            for tt in range(4):
                xv = self.xres[:, tt, cb * 512:(cb + 1) * 512]
                self.I("dve", lambda e, xv=xv, b=banks[tt]: e.tensor_tensor(out=xv, in0=self.ps[b], in1=xv, op=ALU.add),
                       r=[("ps", banks[tt]), ("xres", tt)], w=[("xres", tt)])

    def ffn(self, gcol, gkey, Wg, Wu, Wd):
        self.norm_T(gcol, gkey)
        self.arena_barrier()
        hid = self.arena[:, 0:FT * MT].rearrange("p (f t) -> p f t", f=FT)
        hkeys = [("hT", tt) for tt in range(4)]
        for c in range(FT // 2):
            gv, gk = self.wload(Wg, 0, KT, c * 256, 256)
            uv, uk = self.wload(Wu, 0, KT, c * 256, 256)
            for j in range(2):
                f = c * 2 + j
                bg, bu = self.psbank(), self.psbank()

                def mmw(e, wv=None, j=j, b=None):
                    r = None
                    for k in range(KT):
                        r = e.matmul(self.ps[b], lhsT=wv[:, k, j * 128:(j + 1) * 128], rhs=self.hT[:, k, :],
                                     start=(k == 0), stop=(k == KT - 1))
                    return r
                self.I("pe", lambda e, j=j, b=bg, wv=gv: mmw(e, wv, j, b), r=[gk] + hkeys, w=[("ps", bg)])
                self.I("pe", lambda e, j=j, b=bu, wv=uv: mmw(e, wv, j, b), r=[uk] + hkeys, w=[("ps", bu)])
                sg = self.sg[f % 2]
                self.I("act", lambda e, sg=sg, b=bg: e.activation(out=sg, in_=self.ps[b], func=AF.Silu),
                       r=[("ps", bg)], w=[("sg", f % 2)])
                self.I("dve", lambda e, sg=sg, b=bu, f=f: e.tensor_tensor(out=hid[:, f, :], in0=self.ps[b], in1=sg, op=ALU.mult),
                       r=[("ps", bu), ("sg", f % 2), "arena_own"], w=[("hid", f)])
        hk = [("hid", f) for f in range(FT)] + ["arena_own"]
        self.proj_out(hid, hk, FT, Wd, kc=8)

    def load_rows(self, src, mt):
        for tt in range(4):
            r0 = mt * MT + tt * 128
            self.I("sp", lambda e, tt=tt, r0=r0: e.dma_start(out=self.xres[:, tt, :], in_=src[r0:r0 + 128, :]),
                   w=[("xres", tt)], dma=True)

    def store_rows(self, dst, mt, wkey=None):
        outs = []
        for tt in range(4):
            r0 = mt * MT + tt * 128
            outs.append(self.I("sp", lambda e, tt=tt, r0=r0: e.dma_start(out=dst[r0:r0 + 128, :], in_=self.xres[:, tt, :]),
                               r=[("xres", tt)], w=([wkey] if wkey else []), dma=True))
        return outs


def phase_A(B, io):
    I = B.I
    x_in, xh_in = io["xs"], io["xh"]
    Wqkv, Wo = io["wqkv"], io["wo"]
    x_out = io["x1"]

    gm = B.load_cols("gm0", io["gm0"], KT)
    gf = B.load_cols("gf0", io["gf0"], KT)
    gq = B.load_cols("gq", io["gq"], 64)
    gk = B.load_cols("gk", io["gk"], 64)
    snk = B.load_cols("snk", io["snk"], 32)
    hv = B.load_cols("hv", io["hv"], 1)
    NT1 = TOK // 128 + 1
    posi = B.load_cols("posi", io["pos"], NT1, I32)

    posf = B.sb("posf", [128, NT1], F32)
    invf = B.sb("invf", [128, 32], F32)
    AO = "arena_own"
    a32 = B.arena.bitcast(F32)
    def atmp(i):
        return a32[:, 8192 + i * NT1 * 32:8192 + (i + 1) * NT1 * 32].rearrange("p (t c) -> p t c", t=NT1)
    ang, ang2, kf, msk = atmp(0), atmp(1), atmp(2), atmp(3)
    kq = atmp(4).bitcast(I32)
    sin_t = B.sb("sin_t", [128, NT1, 32], F32)
    cos_t = B.sb("cos_t", [128, NT1, 32], F32)
    I("dve", lambda e: e.tensor_copy(out=posf, in_=posi), r=["posi"], w=["posf"])

    def mk_invf(e):
        r = None
        for i in range(32):
            r = e.memset(invf[:, i:i + 1], float(np.float32(10000.0) ** np.float32(-(2.0 * i) / 64.0)))
        return r
    I("pool", mk_invf, w=["invf"])
    C1 = 6.28125
    C2 = TWO_PI - C1
    V = lambda fn, r, w: I("dve", fn, list(r) + [AO], w)
    V(lambda e: e.tensor_tensor(out=ang, in0=posf.unsqueeze(2).to_broadcast([128, NT1, 32]),
                                in1=invf.unsqueeze(1).to_broadcast([128, NT1, 32]), op=ALU.mult), ["posf", "invf"], ["ang"])
    V(lambda e: e.tensor_scalar(out=kq, in0=ang, scalar1=1.0 / TWO_PI, scalar2=None, op0=ALU.mult), ["ang"], ["kq"])
    V(lambda e: e.tensor_copy(out=kf, in_=kq), ["kq"], ["kf"])
    V(lambda e: e.scalar_tensor_tensor(out=ang2, in0=kf, scalar=-C1, in1=ang, op0=ALU.mult, op1=ALU.add), ["kf", "ang"], ["ang2"])
    V(lambda e: e.scalar_tensor_tensor(out=ang, in0=kf, scalar=-C2, in1=ang2, op0=ALU.mult, op1=ALU.add), ["kf", "ang2"], ["ang"])
    V(lambda e: e.tensor_scalar(out=msk, in0=ang, scalar1=math.pi, scalar2=-TWO_PI, op0=ALU.is_gt, op1=ALU.mult), ["ang"], ["msk"])
    V(lambda e: e.tensor_tensor(out=ang2, in0=ang, in1=msk, op=ALU.add), ["ang", "msk"], ["ang2"])
    V(lambda e: e.tensor_scalar(out=msk, in0=ang2, scalar1=-math.pi, scalar2=TWO_PI, op0=ALU.is_lt, op1=ALU.mult), ["ang2"], ["msk"])
    V(lambda e: e.tensor_tensor(out=ang, in0=ang2, in1=msk, op=ALU.add), ["ang2", "msk"], ["ang"])
    V(lambda e: e.tensor_scalar(out=kf, in0=ang, scalar1=math.pi / 2, scalar2=None, op0=ALU.add), ["ang"], ["kf"])
    V(lambda e: e.tensor_scalar(out=msk, in0=kf, scalar1=math.pi, scalar2=-TWO_PI, op0=ALU.is_gt, op1=ALU.mult), ["kf"], ["msk"])
    V(lambda e: e.tensor_tensor(out=ang2, in0=kf, in1=msk, op=ALU.add), ["kf", "msk"], ["ang2"])
    PI_S = 3.1415925
    V(lambda e: e.tensor_scalar(out=kf, in0=ang, scalar1=PI_S, scalar2=-PI_S, op0=ALU.min, op1=ALU.max), ["ang"], ["kf"])
    V(lambda e: e.tensor_scalar(out=msk, in0=ang2, scalar1=PI_S, scalar2=-PI_S, op0=ALU.min, op1=ALU.max), ["ang2"], ["msk"])
    I("act", lambda e: e.activation(out=sin_t, in_=kf, func=AF.Sin), r=["kf", AO], w=["sin_t"])
    I("act", lambda e: e.activation(out=cos_t, in_=msk, func=AF.Sin), r=["msk", AO], w=["cos_t"])
    V = lambda fn, r, w: I("dve", fn, r, w)

    negM = B.sb("negM", [128, 1], F32)
    mtmp = B.sb("mtmp", [128, 4], F32)
    sinkexp = B.sb("sinkexp", [128, 32], F32)
    V(lambda e: e.tensor_reduce(out=mtmp[:, 0:1], in_=gq, axis=AX.X, op=ALU.max, apply_absolute_value=True), ["gq"], ["mt0"])
    V(lambda e: e.tensor_reduce(out=mtmp[:, 1:2], in_=gk, axis=AX.X, op=ALU.max, apply_absolute_value=True), ["gk"], ["mt1"])
    V(lambda e: e.tensor_tensor(out=mtmp[:, 2:3], in0=mtmp[:, 0:1], in1=mtmp[:, 1:2], op=ALU.mult), ["mt0", "mt1"], ["mt2"])
    V(lambda e: e.tensor_scalar(out=negM, in0=mtmp[:, 2:3], scalar1=-8.0, scalar2=None, op0=ALU.mult), ["mt2"], ["negM"])
    I("act", lambda e: e.activation(out=sinkexp, in_=snk, func=AF.Exp, bias=negM, scale=1.0), r=["snk", "negM"], w=["sinkexp"])

    onesw = B.sb("onesw", [128, 512], BF16)
    maskP = B.sb("maskP", [128, 512], BF16)
    maskO = B.sb("maskO", [128, 512], BF16)
    maskP0 = B.sb("maskP0", [128, 512], BF16)
    I("pool", lambda e: e.memset(onesw, 1.0), w=["onesw"])
    I("pool", lambda e: e.affine_select(out=maskP, in_=onesw, pattern=[[0, 4], [-1, 128]], compare_op=ALU.is_gt,
                                        fill=0.0, base=0, channel_multiplier=1), r=["onesw"], w=["maskP"])
    I("pool", lambda e: e.affine_select(out=maskO, in_=onesw, pattern=[[0, 4], [1, 128]], compare_op=ALU.is_ge,
                                        fill=0.0, base=0, channel_multiplier=-1), r=["onesw"], w=["maskO"])
    V(lambda e: e.tensor_scalar(out=maskP0, in0=maskP, scalar1=hv[:, 0:1], scalar2=None, op0=ALU.mult), ["maskP", "hv"], ["maskP0"])

    sq = B.sb("sq", [128, 512], F32)
    hss = B.sb("hss", [128, 8], F32)
    hrs = B.sb("hrs", [128, 8], F32)
    qn = B.sb("qn", [128, 512], F32)
    qn2 = sq
    rtmp = B.sb("rtmp", [128, 4, 8, 32], F32)
    kdup = B.sb("kdup", [128, 4, 2, 64], BF16)
    qT = B.sb("qT", [128, 16, 128], BF16)
    kT5 = B.sb("kT5", [128, 5, 4, 128], BF16)
    V5 = B.sb("V5", [128, 5, 256], BF16)
    PT = [B.sb("PT%d" % i, [128, 512], BF16) for i in range(4)]
    den = B.sb("den", [128, 512], F32)
    rden = den
    hTh = B.arena[:, 4096:4096 + KT * 128].rearrange("p (k t) -> p k t", k=KT)
    xhalo = a32[:, 0:D].rearrange("p (o c) -> p o c", o=1)
    attnT = B.arena[:, 0:KT * MT].rearrange("p (k t) -> p k t", k=KT)
    qrot4 = B.arena[:, KT * MT:KT * MT + 4 * 2048].rearrange("p (t c) -> p t c", t=4)

    def qk_post(b, nh, gt, gvec, gkey, out_ap, outkey, extra_r=()):
        pv = B.ps[b][:, 0:nh * 64]
        p3 = pv.rearrange("p (h d) -> p h d", h=nh)
        I("act", lambda e: e.activation(out=sq[:, 0:nh * 64], in_=pv, func=AF.Square), r=[("ps", b)], w=["sq"])
        V(lambda e: e.tensor_reduce(out=hss[:, 0:nh], in_=sq[:, 0:nh * 64].rearrange("p (h d) -> p h d", h=nh),
                                    axis=AX.X, op=ALU.add), ["sq"], ["hss"])
        I("act", lambda e: e.activation(out=hrs[:, 0:nh], in_=hss[:, 0:nh], func=AF.Sqrt, bias=B.epsb, scale=1.0 / 64),
          r=["hss", "epsb"], w=["hrs"])
        V(lambda e: e.reciprocal(out=hss[:, 0:nh], in_=hrs[:, 0:nh]), ["hrs"], ["hss2"])
        q3 = qn[:, 0:nh * 64].rearrange("p (h d) -> p h d", h=nh)
        q32 = qn2[:, 0:nh * 64].rearrange("p (h d) -> p h d", h=nh)
        cs = cos_t[:, gt, :].unsqueeze(1).to_broadcast([128, nh, 32])
        sn = sin_t[:, gt, :].unsqueeze(1).to_broadcast([128, nh, 32])
        t1, t2 = q32[:, :, 0:32], q32[:, :, 32:64]
        ra, rb, rc, rd = (rtmp[:, i, 0:nh, :] for i in range(4))
        V(lambda e: e.tensor_tensor(out=q3, in0=p3, in1=hss[:, 0:nh].unsqueeze(2).to_broadcast([128, nh, 64]), op=ALU.mult),
          [("ps", b), "hss2"], ["qn"])
        V(lambda e: e.tensor_tensor(out=q32, in0=q3, in1=gvec.unsqueeze(1).to_broadcast([128, nh, 64]), op=ALU.mult),
          ["qn", gkey], ["sq"])
        V(lambda e: e.tensor_tensor(out=ra, in0=t1, in1=cs, op=ALU.mult), ["sq", "cos_t"], ["ra"])
        V(lambda e: e.tensor_tensor(out=rb, in0=t2, in1=sn, op=ALU.mult), ["sq", "sin_t"], ["rb"])
        V(lambda e: e.tensor_tensor(out=rc, in0=t2, in1=cs, op=ALU.mult), ["sq", "cos_t"], ["rc"])
        V(lambda e: e.tensor_tensor(out=rd, in0=t1, in1=sn, op=ALU.mult), ["sq", "sin_t"], ["rd"])
        V(lambda e: e.tensor_tensor(out=out_ap[:, :, 0:32], in0=ra, in1=rb, op=ALU.subtract), ["ra", "rb"] + list(extra_r), [outkey + ("a",)])
        V(lambda e: e.tensor_tensor(out=out_ap[:, :, 32:64], in0=rc, in1=rd, op=ALU.add), ["rc", "rd"] + list(extra_r), [outkey + ("b",)])

    def kv_finish(b, gt, slot):
        I("act", lambda e: e.activation(out=V5[:, slot, :], in_=B.ps[b][:, 256:512], func=AF.Copy), r=[("ps", b)], w=[("V", slot)])
        qk_post(b, 4, gt, gk, "gk", kdup[:, :, 0, :], ("kdup0",))
        I("act", lambda e: e.activation(out=kdup[:, :, 1, :], in_=kdup[:, :, 0, :], func=AF.Copy),
          r=[("kdup0", "a"), ("kdup0", "b")], w=["kdup1"])
        pt, ptk = B.ptbank()

        def tr(e):
            r = None
            for kv in range(4):
                r = e.transpose(out=pt[:, kv * 128:(kv + 1) * 128], in_=kdup[:, kv, :, :].rearrange("p a d -> p (a d)"),
                                identity=B.ident)
            return r
        I("pe", tr, r=[("kdup0", "a"), ("kdup0", "b"), "kdup1", "ident"], w=[ptk])
        I("act", lambda e: e.activation(out=kT5[:, slot, :, :].rearrange("p k t -> p (k t)"), in_=pt[:, 0:512], func=AF.Copy),
          r=[ptk], w=[("kT", slot)])

    I("sp", lambda e: e.dma_start(out=xhalo[:, 0, :], in_=xh_in), r=[AO], w=[("xhalo", 0)], dma=True)
    B.norm_T(gm, "gm0", ntt=1, src=xhalo, srckeys=[("xhalo", 0)], dst=hTh, dstkey="hTh", xr=[AO])
    bh = B.psbank()
    for ci in range(2):
        wv, wk = B.wload(Wqkv, ci * 8 * 128, 8, 2048, 512)

        def mm(e, wv=wv, ci=ci):
            r = None
            for kk in range(8):
                k = ci * 8 + kk
                r = e.matmul(B.ps[bh], lhsT=hTh[:, k, :], rhs=wv[:, kk, :], start=(k == 0), stop=(k == KT - 1))
            return r
        I("pe", mm, r=[wk, ("hTh", 0), AO], w=[("ps", bh)])
    kv_finish(bh, 0, 0)

    outs = []
    for mt in range(NMT):
        B.load_rows(x_in, mt)
        B.norm_T(gm, "gm0")
        B.arena_barrier()
        for cb in (4, 0, 1, 2, 3):
            banks = [B.psbank() for tt in range(4)]
            for ci in range(2):
                wv, wk = B.wload(Wqkv, ci * 8 * 128, 8, cb * 512, 512)
                for tt in range(4):
                    def mm(e, wv=wv, ci=ci, tt=tt, b=banks[tt]):
                        r = None
                        for kk in range(8):
                            k = ci * 8 + kk
                            r = e.matmul(B.ps[b], lhsT=B.hT[:, k, tt * 128:(tt + 1) * 128], rhs=wv[:, kk, :],
                                         start=(k == 0), stop=(k == KT - 1))
                        return r
                    I("pe", mm, r=[wk, ("hT", tt)], w=[("ps", banks[tt])])
            for tt in range(4):
                gt = mt * 4 + tt + 1
                if cb == 4:
                    kv_finish(banks[tt], gt, tt + 1)
                else:
                    qk_post(banks[tt], 8, gt, gq, "gq",
                            qrot4[:, tt, cb * 512:(cb + 1) * 512].rearrange("p (h d) -> p h d", h=8),
                            ("qrot", tt, cb), extra_r=["arena_own"])
        for tt in range(4):
            qk_keys = [("qrot", tt, cb, ab) for cb in range(4) for ab in ("a", "b")]
            for half in range(2):
                pt, ptk = B.ptbank()

                def tr(e, pt=pt, half=half, tt=tt):
                    r = None
                    for j in range(8):
                        blk = half * 8 + j
                        r = e.transpose(out=pt[:, j * 128:(j + 1) * 128], in_=qrot4[:, tt, blk * 128:(blk + 1) * 128], identity=B.ident)
                    return r
                I("pe", tr, r=qk_keys + ["ident", "arena_own"], w=[ptk])
                I("act", lambda e, pt=pt, half=half: e.activation(
                    out=qT[:, half * 8:(half + 1) * 8, :].rearrange("p k t -> p (k t)"), in_=pt, func=AF.Copy),
                  r=[ptk], w=[("qT", half)])
            for kv in range(4):
                sb_ = [B.psbank() for _ in range(4)]
                bo, bd = B.psbank(), B.psbank()
                idx = 0
                for blk, slot in ((0, tt), (1, tt + 1)):
                    for par in range(2):
                        b = sb_[idx]
                        I("pe", lambda e, b=b, par=par, slot=slot, kv=kv: e.matmul(
                            B.ps[b], lhsT=kT5[par * 64:(par + 1) * 64, slot, kv, :],
                            rhs=qT[par * 64:(par + 1) * 64, 4 * kv:4 * kv + 4, :].rearrange("p k t -> p (k t)"),
                            start=True, stop=True),
                          r=[("kT", slot), ("qT", kv // 2)], w=[("ps", b)])
                        I("act", lambda e, b=b, idx=idx: e.activation(out=PT[idx], in_=B.ps[b], func=AF.Exp, bias=negM, scale=0.125),
                          r=[("ps", b), "negM"], w=[("PTr", idx)])
                        if blk == 0:
                            mk, mkk = (maskP0, "maskP0") if (mt == 0 and tt == 0) else (maskP, "maskP")
                        else:
                            mk, mkk = maskO, "maskO"
                        V(lambda e, idx=idx, mk=mk: e.tensor_tensor(out=PT[idx], in0=PT[idx], in1=mk, op=ALU.mult),
                          [("PTr", idx), mkk], [("PT", idx)])
                        idx += 1
                for par in range(2):
                    for blk, slot in ((0, tt), (1, tt + 1)):
                        idx = blk * 2 + par
                        I("pe", lambda e, par=par, slot=slot, idx=idx, blk=blk, kv=kv, bo=bo: e.matmul(
                            B.ps[bo][par * 64:(par + 1) * 64, :], lhsT=V5[:, slot, kv * 64:(kv + 1) * 64], rhs=PT[idx],
                            start=(blk == 0), stop=(blk == 1)),
                          r=[("V", slot), ("PT", idx)], w=[("ps", bo)])
                    for blk, slot in ((0, tt), (1, tt + 1)):
                        idx = blk * 2 + par
                        I("pe", lambda e, par=par, idx=idx, blk=blk, bd=bd: e.matmul(
                            B.ps[bd][par * 64:(par + 1) * 64, :], lhsT=B.ones[:, 0:64], rhs=PT[idx],
                            start=(blk == 0), stop=(blk == 1)),
                          r=["ones", ("PT", idx)], w=[("ps", bd)])
                for par in range(2):
                    for pair in range(4):
                        h = kv * 8 + 2 * pair + par
                        V(lambda e, par=par, pair=pair, h=h, bd=bd: e.tensor_scalar(
                            out=den[par * 64:(par + 1) * 64, pair * 128:(pair + 1) * 128],
                            in0=B.ps[bd][par * 64:(par + 1) * 64, pair * 128:(pair + 1) * 128],
                            scalar1=sinkexp[par * 64:(par + 1) * 64, h:h + 1], scalar2=None, op0=ALU.add),
                          [("ps", bd), "sinkexp"], [("den", par, pair)])
                dk = [("den", par, pair) for par in range(2) for pair in range(4)]
                V(lambda e: e.reciprocal(out=rden, in_=den), dk, ["rden"] + dk)
                V(lambda e, kv=kv, tt=tt, bo=bo: e.tensor_tensor(
                    out=attnT[:, 4 * kv:4 * kv + 4, tt * 128:(tt + 1) * 128],
                    in0=B.ps[bo].rearrange("p (k t) -> p k t", k=4), in1=rden.rearrange("p (k t) -> p k t", k=4), op=ALU.mult),
                  [("ps", bo), "rden", "arena_own"], [("attnT", tt, kv)])
        I("act", lambda e: e.activation(out=kT5[:, 0, :, :], in_=kT5[:, 4, :, :], func=AF.Copy), r=[("kT", 4)], w=[("kT", 0)])
        I("act", lambda e: e.activation(out=V5[:, 0, :], in_=V5[:, 4, :], func=AF.Copy), r=[("V", 4)], w=[("V", 0)])
        ak = [("attnT", tt, kv) for tt in range(4) for kv in range(4)] + ["arena_own"]
        B.proj_out(attnT, ak, KT, Wo)
        B.ffn(gf, "gf0", io["wg"], io["wu"], io["wd"])
        outs += B.store_rows(x_out, mt)
    return outs


def phase_B(B, io, full):
    I = B.I
    AO = "arena_own"
    B.nrot = 3
    V = lambda fn, r, w: I("dve", fn, r, w)
    A = lambda fn, r, w: I("act", fn, r, w)
    x_in, xh_in, Win = io["x1"], io["x1h"], io["win"]
    gm = B.load_cols("gm1", io["gm1"], KT)
    hv = B.load_cols("hvb", io["hv"], 1)
    cw = B.load_cols("cw", io["cw"], 48 * 4)
    cbias = B.load_cols("cbias", io["cb"], 48)
    dtb = B.load_cols("dtb", io["dtb"], 64)
    alog = B.load_cols("alog", io["alog"], 64)
    if full:
        dcol = B.load_cols("dcol", io["dcol"], 32)
        ngcol = B.load_cols("ngcol", io["ngcol"], 32)
        gf = B.load_cols("gf1", io["gf1"], KT)
        inc = B.load_cols("inc", io["inc"], 4)

    onesf = B.sb("onesf", [128, 256], F32)
    trif = B.sb("trif", [128, 128], F32)
    tri0 = B.sb("tri0", [128, 256], F32)
    tri1 = B.sb("tri1", [128, 256], F32)
    a_t = B.sb("a_t", [128, 64], F32)
    I("pool", lambda e: e.memset(onesf, 1.0), w=["onesf"])
    I("pool", lambda e: e.affine_select(out=trif, in_=onesf[:, 0:128], pattern=[[1, 128]], compare_op=ALU.is_ge, fill=0.0,
                                        base=0, channel_multiplier=-1), r=["onesf"], w=["trif"])
    I("pool", lambda e: e.affine_select(out=tri0, in_=onesf, pattern=[[1, 256]], compare_op=ALU.is_ge, fill=0.0,
                                        base=0, channel_multiplier=-1), r=["onesf"], w=["tri0"])
    I("pool", lambda e: e.affine_select(out=tri1, in_=onesf, pattern=[[1, 256]], compare_op=ALU.is_ge, fill=0.0,
                                        base=-128, channel_multiplier=-1), r=["onesf"], w=["tri1"])
    A(lambda e: e.activation(out=a_t, in_=alog, func=AF.Exp), ["alog"], ["a_e"])
    V(lambda e: e.tensor_scalar(out=a_t, in0=a_t, scalar1=-1.0, scalar2=None, op0=ALU.mult), ["a_e"], ["a_t"])

    st = B.sb("st", [128, 8, 512], F32)
    stbf = B.sb("stbf", [128, 512], BF16)
    carry = B.sb("carry", [128, 48, 3], F32)
    dtraw = B.sb("dtraw", [128, 4, 64], F32)
    dtv = B.sb("dtv", [128, 4, 64], F32)
    dtmp = B.sb("dtmp", [128, 4, 64], F32)
    adt = B.sb("adt", [128, 4, 64], F32)
    acum = B.sb("acum", [128, 4, 64], F32)
    te = B.sb("te", [128, 4, 64], F32)
    tot = B.sb("tot", [128, 2, 64], F32)
    dec = B.sb("dec", [128, 2, 64], F32)
    segtot = B.sb("segtot", [128, 64], F32)
    ubuf = B.sb("ubuf", [128, 515], F32)
    acc = [B.sb("acc%d" % i, [128, 512], F32) for i in range(2)]
    BT = B.sb("BT", [128, 512], BF16)
    Btm = B.sb("Btm", [128, 4, 128], BF16)
    xdt = B.sb("xdt", [128, 4, 512], BF16)
    xdte = B.sb("xdte", [128, 2, 512], BF16)
    hTh = B.sb("hThb", [128, KT, 4], BF16)
    xT_g = B.arena[:, 16384:16384 + 2048].rearrange("p (j t) -> p j t", j=4)
    if full:
        CT = B.sb("CT", [128, 512], BF16)
        CBm = [B.sb("CBm%d" % i, [128, 256], BF16) for i in range(2)]
        adtrep = [B.sb("adtrep%d" % i, [128, 128], F32) for i in range(2)]
        darg = [B.sb("darg%d" % i, [128, 256], F32) for i in range(2)]
        Mt = [B.sb("Mt%d" % i, [128, 256], BF16) for i in range(2)]
        eoff = B.sb("eoff", [128, 256], F32)
        roff = B.sb("roff", [128, 256], BF16)
        sqn = B.sb("sqn", [128, 4, 256], BF16)
        rstd = B.sb("rstd", [128, 256], F32)
        ynT = B.arena[:, 0:32 * 512].rearrange("p (k t) -> p k t", k=32)
        zs = B.arena[:, 18432:18432 + 2048].rearrange("p (j t) -> p j t", j=4)
        yg = B.arena.bitcast(F32)[:, 10240:10240 + 1024].rearrange("p (j t) -> p j t", j=4)
    I("pool", lambda e: e.memset(segtot, 0.0), w=["segtot"])

    if not full:
        I("pool", lambda e: e.memset(st.rearrange("p g c -> p (g c)"), 0.0), w=[("st", g) for g in range(8)])
    else:
        sS = B.arena.bitcast(F32)[:, 0:4096]
        sD = B.sb("sD", [128, 64], F32)
        fr = B.sb("fr", [128, 64], F32)
        I("pool", lambda e: e.memset(st.rearrange("p g c -> p (g c)"), 0.0), w=["stall"])
        for r in range(3):
            I("sp", lambda e, r=r: e.dma_start(out=sS, in_=io["stS"][r]), r=[AO], w=["sS"], dma=True)
            I("sp", lambda e, r=r: e.dma_start(out=sD, in_=io["stD"][r]), w=["sD"], dma=True)
            V(lambda e, r=r: e.tensor_scalar(out=fr, in0=sD, scalar1=-1.0, scalar2=inc[:, r:r + 1], op0=ALU.add, op1=ALU.mult),
              ["sD", "inc"], ["fr0"])
            V(lambda e: e.tensor_scalar(out=fr, in0=fr, scalar1=1.0, scalar2=None, op0=ALU.add), ["fr0"], ["fr"])
            V(lambda e: e.tensor_tensor(out=st.rearrange("p g (r c) -> p (g r) c", r=8), in0=st.rearrange("p g (r c) -> p (g r) c", r=8),
                                        in1=fr.unsqueeze(2).to_broadcast([128, 64, 64]), op=ALU.mult), ["stall", "fr"], ["stall"])
            V(lambda e, r=r: e.scalar_tensor_tensor(out=st.rearrange("p g c -> p (g c)"), in0=sS, scalar=inc[:, r:r + 1],
                                                    in1=st.rearrange("p g c -> p (g c)"), op0=ALU.mult, op1=ALU.add),
              ["stall", "sS", "inc", AO], ["stall"])
        V(lambda e: e.memset(B.stat[:, 5:6], 0.0), ["stall"], [("st", g) for g in range(8)])

    xhalo = B.arena.bitcast(F32)[:, 0:D].rearrange("p (o c) -> p o c", o=1)
    hThf = B.arena[:, 4096:4096 + KT * 128].rearrange("p (k t) -> p k t", k=KT)
    I("sp", lambda e: e.dma_start(out=xhalo[:, 0, :], in_=xh_in), r=[AO], w=[("xhalo", 0)] + (["sS"] if full else []), dma=True)
    B.norm_T(gm, "gm1", ntt=1, src=xhalo, srckeys=[("xhalo", 0)], dst=hThf, dstkey="hThf", xr=[AO])
    V(lambda e: e.tensor_copy(out=hTh, in_=hThf[:, :, 124:128]), [("hThf", 0), AO], ["hTh"])

    def in_tile(wv, wk, j, b, mt, halo_bank=None):
        def mm(e):
            r = None
            for k in range(KT):
                r = e.matmul(B.ps[b], lhsT=wv[:, k, j * 128:(j + 1) * 128], rhs=B.hT[:, k, :], start=(k == 0), stop=(k == KT - 1))
            return r
        I("pe", mm, r=[wk] + [("hT", tt) for tt in range(4)], w=[("ps", b)])
        if halo_bank is not None:
            def mmh(e):
                r = None
                for k in range(KT):
                    r = e.matmul(B.ps[halo_bank][:, 0:4], lhsT=wv[:, k, j * 128:(j + 1) * 128], rhs=hTh[:, k, :],
                                 start=(k == 0), stop=(k == KT - 1))
                return r
            I("pe", mmh, r=[wk, "hTh"], w=[("ps", halo_bank)])

    def conv_tile(b, ct, mt, out_ap, outkeys, halo_bank, extra_r=()):
        if mt == 0:
            V(lambda e: e.tensor_scalar(out=carry[:, ct, :], in0=B.ps[halo_bank][:, 1:4], scalar1=hv[:, 0:1], scalar2=None, op0=ALU.mult),
              [("ps", halo_bank), "hvb"], [("carry", ct)])
        A(lambda e: e.activation(out=ubuf[:, 0:3], in_=carry[:, ct, :], func=AF.Copy), [("carry", ct)], ["ubuf_h"])
        A(lambda e: e.activation(out=ubuf[:, 3:515], in_=B.ps[b], func=AF.Copy), [("ps", b)], ["ubuf_m"])
        V(lambda e: e.tensor_copy(out=carry[:, ct, :], in_=ubuf[:, 512:515]), ["ubuf_m"], [("carry", ct)])
        a0, a1 = acc
        V(lambda e: e.tensor_scalar(out=a0, in0=ubuf[:, 0:512], scalar1=cw[:, ct * 4:ct * 4 + 1], scalar2=None, op0=ALU.mult),
          ["ubuf_h", "ubuf_m", "cw"], ["acc0"])
        V(lambda e: e.scalar_tensor_tensor(out=a1, in0=ubuf[:, 1:513], scalar=cw[:, ct * 4 + 1:ct * 4 + 2], in1=a0, op0=ALU.mult, op1=ALU.add),
          ["ubuf_h", "ubuf_m", "cw", "acc0"], ["acc1"])
        V(lambda e: e.scalar_tensor_tensor(out=a0, in0=ubuf[:, 2:514], scalar=cw[:, ct * 4 + 2:ct * 4 + 3], in1=a1, op0=ALU.mult, op1=ALU.add),
          ["ubuf_h", "ubuf_m", "cw", "acc1"], ["acc0"])
        V(lambda e: e.scalar_tensor_tensor(out=a1, in0=ubuf[:, 3:515], scalar=cw[:, ct * 4 + 3:ct * 4 + 4], in1=a0, op0=ALU.mult, op1=ALU.add),
          ["ubuf_m", "cw", "acc0"], ["acc1"])
        A(lambda e: e.activation(out=out_ap, in_=a1, func=AF.Silu, bias=cbias[:, ct:ct + 1], scale=1.0),
          ["acc1", "cbias"] + list(extra_r), list(outkeys))

    outs = []
    for mt in range(NMT):
        B.load_rows(x_in, mt)
        B.norm_T(gm, "gm1")
        B.arena_barrier()
        wv, wk = B.wload(Win, 0, KT, 10240, 64)
        for tt in range(4):
            b = B.psbank()

            def mm(e, b=b, tt=tt, wv=wv):
                r = None
                for k in range(KT):
                    r = e.matmul(B.ps[b][:, 0:64], lhsT=B.hT[:, k, tt * 128:(tt + 1) * 128], rhs=wv[:, k, :], start=(k == 0), stop=(k == KT - 1))
                return r
            I("pe", mm, r=[wk, ("hT", tt)], w=[("ps", b)])
            V(lambda e, b=b, tt=tt: e.tensor_tensor(out=dtraw[:, tt, :], in0=B.ps[b][:, 0:64], in1=dtb, op=ALU.add), [("ps", b), "dtb"], [("dtraw", tt)])
        dk = [("dtraw", tt) for tt in range(4)]
        V(lambda e: e.scalar_tensor_tensor(out=dtmp, in0=dtraw, scalar=-1.0, in1=dtraw, op0=ALU.mult, op1=ALU.min), dk, ["dtmp"])
        A(lambda e: e.activation(out=dtmp, in_=dtmp, func=AF.Exp), ["dtmp"], ["dtmp"])
        A(lambda e: e.activation(out=dtmp, in_=dtmp, func=AF.Ln, bias=B.oneb, scale=1.0), ["dtmp", "oneb"], ["dtmp"])
        V(lambda e: e.scalar_tensor_tensor(out=dtv, in0=dtraw, scalar=0.0, in1=dtmp, op0=ALU.max, op1=ALU.add), dk + ["dtmp"], ["dtv"])
        V(lambda e: e.tensor_tensor(out=adt, in0=dtv, in1=a_t.unsqueeze(1).to_broadcast([128, 4, 64]), op=ALU.mult), ["dtv", "a_t"], ["adt"])
        for c in range(2):
            t0, t1 = 2 * c, 2 * c + 1
            b0, b1, b2 = B.psbank(), B.psbank(), B.psbank()
            I("pe", lambda e, b0=b0, t0=t0: e.matmul(B.ps[b0][:, 0:64], lhsT=trif, rhs=adt[:, t0, :], start=True, stop=True),
              r=["trif", "adt"], w=[("ps", b0)])

            def mm1(e, b1=b1, t0=t0, t1=t1):
                e.matmul(B.ps[b1][:, 0:64], lhsT=onesf[:, 0:128], rhs=adt[:, t0, :], start=True, stop=False)
                return e.matmul(B.ps[b1][:, 0:64], lhsT=trif, rhs=adt[:, t1, :], start=False, stop=True)
            I("pe", mm1, r=["trif", "onesf", "adt"], w=[("ps", b1)])

            def mm2(e, b2=b2, t0=t0, t1=t1):
                e.matmul(B.ps[b2][:, 0:64], lhsT=onesf[:, 0:128], rhs=adt[:, t0, :], start=True, stop=False)
                return e.matmul(B.ps[b2][:, 0:64], lhsT=onesf[:, 0:128], rhs=adt[:, t1, :], start=False, stop=True)
            I("pe", mm2, r=["onesf", "adt"], w=[("ps", b2)])
            A(lambda e, b0=b0, t0=t0: e.activation(out=acum[:, t0, :], in_=B.ps[b0][:, 0:64], func=AF.Copy), [("ps", b0)], [("acum", t0)])
            A(lambda e, b1=b1, t1=t1: e.activation(out=acum[:, t1, :], in_=B.ps[b1][:, 0:64], func=AF.Copy), [("ps", b1)], [("acum", t1)])
            A(lambda e, b2=b2, c=c: e.activation(out=tot[:, c, :], in_=B.ps[b2][:, 0:64], func=AF.Copy), [("ps", b2)], [("tot", c)])
            A(lambda e, b2=b2, c=c: e.activation(out=dec[:, c, :], in_=B.ps[b2][:, 0:64], func=AF.Exp), [("ps", b2)], [("dec", c)])
            V(lambda e, c=c: e.tensor_tensor(out=segtot, in0=segtot, in1=tot[:, c, :], op=ALU.add), [("tot", c), "segtot"], ["segtot"])
            for t in (t0, t1):
                V(lambda e, t=t, c=c: e.tensor_tensor(out=te[:, t, :], in0=tot[:, c, :], in1=acum[:, t, :], op=ALU.subtract),
                  [("tot", c), ("acum", t)], [("te0", t)])
                A(lambda e, t=t: e.activation(out=te[:, t, :], in_=te[:, t, :], func=AF.Exp), [("te0", t)], [("te", t)])

        for g in range(8):
            hb = 5 if mt == 0 else None
            if full:
                for c2 in range(2):
                    wv, wk = B.wload(Win, 0, KT, g * 512 + c2 * 256, 256)
                    for j in range(2):
                        b = B.psbank()
                        in_tile(wv, wk, j, b, mt)
                        A(lambda e, b=b, jj=c2 * 2 + j: e.activation(out=zs[:, jj, :], in_=B.ps[b], func=AF.Silu),
                          [("ps", b), AO], [("zs", c2 * 2 + j)])
            for c2 in range(2):
                wv, wk = B.wload(Win, 0, KT, 4096 + g * 512 + c2 * 256, 256)
                for j in range(2):
                    b = B.psbank()
                    jj = c2 * 2 + j
                    in_tile(wv, wk, j, b, mt, hb)
                    conv_tile(b, g * 4 + jj, mt, xT_g[:, jj, :], [("xT", jj)], hb, extra_r=[AO])
            wv, wk = B.wload(Win, 0, KT, 8192 + g * 128, 128)
            b = B.psbank()
            in_tile(wv, wk, 0, b, mt, hb)
            conv_tile(b, 32 + g, mt, BT, ["BT"], hb)
            if full:
                wv, wk = B.wload(Win, 0, KT, 9216 + g * 128, 128)
                b = B.psbank()
                in_tile(wv, wk, 0, b, mt, hb)
                conv_tile(b, 40 + g, mt, CT, ["CT"], hb)
            for tt in range(4):
                pt, ptk = B.ptbank()

                def tr(e, pt=pt, tt=tt):
                    r = None
                    for jj in range(4):
                        r = e.transpose(out=pt[:, jj * 128:(jj + 1) * 128], in_=xT_g[:, jj, tt * 128:(tt + 1) * 128], identity=B.ident)
                    return r
                I("pe", tr, r=[("xT", jj) for jj in range(4)] + ["ident", AO], w=[ptk])
                V(lambda e, pt=pt, tt=tt, g=g: e.tensor_tensor(
                    out=xdt[:, tt, :].rearrange("p (r d) -> p r d", r=8), in0=pt[:, 0:512].rearrange("p (r d) -> p r d", r=8),
                    in1=dtv[:, tt, g * 8:(g + 1) * 8].unsqueeze(2).to_broadcast([128, 8, 64]), op=ALU.mult),
                  [ptk, "dtv"], [("xdt", tt)])
            pt, ptk = B.ptbank()

            def trb(e, pt=pt):
                r = None
                for tt in range(4):
                    r = e.transpose(out=pt[:, tt * 128:(tt + 1) * 128], in_=BT[:, tt * 128:(tt + 1) * 128], identity=B.ident)
                return r
            I("pe", trb, r=["BT", "ident"], w=[ptk])
            A(lambda e, pt=pt: e.activation(out=Btm.rearrange("p t n -> p (t n)"), in_=pt[:, 0:512], func=AF.Copy), [ptk], ["Btm"])

            for c in range(2):
                t0, t1 = 2 * c, 2 * c + 1
                cs = slice(c * 256, (c + 1) * 256)
                if full:
                    A(lambda e, g=g: e.activation(out=stbf, in_=st[:, g, :], func=AF.Copy), [("st", g)], ["stbf"])
                    for j, tj in enumerate((t0, t1)):
                        b = B.psbank()
                        I("pe", lambda e, b=b, tj=tj, cs=cs: e.matmul(B.ps[b][:, 0:256], lhsT=BT[:, tj * 128:(tj + 1) * 128], rhs=CT[:, cs],
                                                                      start=True, stop=True), r=["BT", "CT"], w=[("ps", b)])
                        trj, trk = (tri0, "tri0") if j == 0 else (tri1, "tri1")
                        V(lambda e, b=b, j=j, trj=trj: e.tensor_tensor(out=CBm[j], in0=B.ps[b][:, 0:256], in1=trj, op=ALU.mult),
                          [("ps", b), trk], [("CBm", j)])
                    ybanks = [3, 4]
                    for r in range(8):
                        h = g * 8 + r
                        jj, half = r // 2, r % 2
                        V(lambda e, h=h, t0=t0: e.tensor_copy(out=adtrep[0], in_=adt[:, t0, h:h + 1].to_broadcast([128, 128])), ["adt"], [("adtrep", 0)])
                        V(lambda e, h=h, t1=t1: e.tensor_copy(out=adtrep[1], in_=adt[:, t1, h:h + 1].to_broadcast([128, 128])), ["adt"], [("adtrep", 1)])
                        bb = B.psbank()

                        def mmb(e, bb=bb):
                            e.matmul(B.ps[bb][:, 0:256], lhsT=adtrep[0], rhs=tri0, start=True, stop=False)
                            return e.matmul(B.ps[bb][:, 0:256], lhsT=adtrep[1], rhs=tri1, start=False, stop=True)
                        I("pe", mmb, r=[("adtrep", 0), ("adtrep", 1), "tri0", "tri1"], w=[("ps", bb)])
                        for j, tj in enumerate((t0, t1)):
                            V(lambda e, bb=bb, j=j, tj=tj, h=h: e.tensor_scalar(out=darg[j], in0=B.ps[bb][:, 0:256], scalar1=acum[:, tj, h:h + 1],
                                                                                scalar2=0.0, op0=ALU.subtract, op1=ALU.min),
                              [("ps", bb), ("acum", tj)], [("darg", j)])
                            A(lambda e, j=j: e.activation(out=darg[j], in_=darg[j], func=AF.Exp), [("darg", j)], [("darg", j)])
                            V(lambda e, j=j: e.tensor_tensor(out=Mt[j], in0=darg[j], in1=CBm[j], op=ALU.mult), [("darg", j), ("CBm", j)], [("Mt", j)])
                        A(lambda e, bb=bb: e.activation(out=eoff, in_=B.ps[bb][:, 0:256], func=AF.Exp), [("ps", bb)], ["eoff"])
                        V(lambda e, cs=cs: e.tensor_tensor(out=roff, in0=eoff, in1=CT[:, cs], op=ALU.mult), ["eoff", "CT"], ["roff"])
                        yb = ybanks[jj // 2]
                        yo = B.ps[yb][half * 64:(half + 1) * 64, (jj % 2) * 256:(jj % 2) * 256 + 256]

                        def mmy(e, yo=yo, r=r, t0=t0, t1=t1):
                            e.matmul(yo, lhsT=xdt[:, t0, r * 64:(r + 1) * 64], rhs=Mt[0], start=True, stop=False)
                            e.matmul(yo, lhsT=xdt[:, t1, r * 64:(r + 1) * 64], rhs=Mt[1], start=False, stop=False)
                            return e.matmul(yo, lhsT=stbf[:, r * 64:(r + 1) * 64], rhs=roff, start=False, stop=True)
                        I("pe", mmy, r=[("xdt", t0), ("xdt", t1), ("Mt", 0), ("Mt", 1), "stbf", "roff"], w=[("ps", yb)])
                    for jj in range(4):
                        yb = ybanks[jj // 2]
                        V(lambda e, jj=jj, yb=yb, cs=cs, g=g: e.scalar_tensor_tensor(
                            out=yg[:, jj, :], in0=xT_g[:, jj, cs], scalar=dcol[:, g * 4 + jj:g * 4 + jj + 1],
                            in1=B.ps[yb][:, (jj % 2) * 256:(jj % 2) * 256 + 256], op0=ALU.mult, op1=ALU.add),
                          [("ps", yb), ("xT", jj), "dcol", AO], [("yg", jj)])
                        V(lambda e, jj=jj, cs=cs: e.tensor_tensor(out=yg[:, jj, :], in0=yg[:, jj, :], in1=zs[:, jj, cs], op=ALU.mult),
                          [("yg", jj), ("zs", jj), AO], [("yg", jj)])
                        A(lambda e, jj=jj: e.activation(out=sqn[:, jj, :], in_=yg[:, jj, :], func=AF.Square), [("yg", jj), AO], [("sqn", jj)])
                    bs = B.psbank()

                    def mms(e, bs=bs):
                        r = None
                        for jj in range(4):
                            r = e.matmul(B.ps[bs][:, 0:256], lhsT=B.ones, rhs=sqn[:, jj, :], start=(jj == 0), stop=(jj == 3))
                        return r
                    I("pe", mms, r=["ones"] + [("sqn", jj) for jj in range(4)], w=[("ps", bs)])
                    A(lambda e, bs=bs: e.activation(out=rstd, in_=B.ps[bs][:, 0:256], func=AF.Sqrt, bias=B.epsb, scale=1.0 / 512),
                      [("ps", bs), "epsb"], ["rstd0"])
                    V(lambda e: e.reciprocal(out=rstd, in_=rstd), ["rstd0"], ["rstd"])
                    for jj in range(4):
                        V(lambda e, jj=jj, cs=cs, g=g: e.scalar_tensor_tensor(
                            out=ynT[:, g * 4 + jj, cs], in0=yg[:, jj, :], scalar=ngcol[:, g * 4 + jj:g * 4 + jj + 1], in1=rstd,
                            op0=ALU.mult, op1=ALU.mult), [("yg", jj), "rstd", "ngcol", AO], [("ynT", g * 4 + jj)])
                for i, t in enumerate((t0, t1)):
                    V(lambda e, i=i, t=t, g=g: e.tensor_tensor(
                        out=xdte[:, i, :].rearrange("p (r d) -> p r d", r=8), in0=xdt[:, t, :].rearrange("p (r d) -> p r d", r=8),
                        in1=te[:, t, g * 8:(g + 1) * 8].unsqueeze(2).to_broadcast([128, 8, 64]), op=ALU.mult),
                      [("xdt", t), ("te", t)], [("xdte", i)])
                bn = B.psbank()

                def mmn(e, bn=bn, t0=t0, t1=t1):
                    e.matmul(B.ps[bn], lhsT=Btm[:, t0, :], rhs=xdte[:, 0, :], start=True, stop=False)
                    return e.matmul(B.ps[bn], lhsT=Btm[:, t1, :], rhs=xdte[:, 1, :], start=False, stop=True)
                I("pe", mmn, r=["Btm", ("xdte", 0), ("xdte", 1)], w=[("ps", bn)])
                V(lambda e, g=g, c=c: e.tensor_tensor(
                    out=st[:, g, :].rearrange("p (r d) -> p r d", r=8), in0=st[:, g, :].rearrange("p (r d) -> p r d", r=8),
                    in1=dec[:, c, g * 8:(g + 1) * 8].unsqueeze(2).to_broadcast([128, 8, 64]), op=ALU.mult),
                  [("st", g), ("dec", c)], [("st", g)])
                V(lambda e, g=g, bn=bn: e.tensor_tensor(out=st[:, g, :], in0=st[:, g, :], in1=B.ps[bn], op=ALU.add),
                  [("st", g), ("ps", bn)], [("st", g)])
        if full:
            yk = [("ynT", k) for k in range(32)] + [AO]
            B.nrot = 6
            B.proj_out(ynT, yk, 32, io["wout"])
            B.ffn(gf, "gf1", io["wg"], io["wu"], io["wd"])
            outs += B.store_rows(io["y"], mt)
            B.nrot = 3
    if not full:
        A(lambda e: e.activation(out=segtot, in_=segtot, func=AF.Exp), ["segtot"], ["segdec"])
        outs.append(I("sp", lambda e: e.dma_start(out=io["stS_out"], in_=st.rearrange("p g c -> p (g c)")),
                      r=[("st", g) for g in range(8)], dma=True))
        outs.append(I("sp", lambda e: e.dma_start(out=io["stD_out"], in_=segtot), r=["segdec"], dma=True))
    return outs


def _col(v, n):
    return np.ascontiguousarray(np.asarray(v, np.float32).reshape(n, 128).T)


def _tile(v):
    v = np.asarray(v)
    return np.ascontiguousarray(np.broadcast_to(v[None, :], (128, v.shape[0])))


def build_A():
    B = Builder()
    io = {}
    for name, shape, dt in (("xs", [TOK, D], F32), ("xh", [128, D], F32), ("pos", [128, TOK // 128 + 1], I32), ("hv", [128, 1], F32),
                            ("gm0", [128, KT], F32), ("gf0", [128, KT], F32), ("gq", [128, 64], F32), ("gk", [128, 64], F32),
                            ("snk", [128, 32], F32), ("wqkv", [D, QKVW], F32), ("wo", [D, D], F32),
                            ("wg", [D, DFF], F32), ("wu", [D, DFF], F32), ("wd", [DFF, D], F32)):
        io[name] = B.dram(name, shape, dt, "ExternalInput")
    io["x1"] = B.dram("x1", [TOK, D], F32, "ExternalOutput")
    outs = phase_A(B, io)
    B.S.emit(final_wait_ops=outs)
    return B.nc


def inputs_A(inp):
    x = np.asarray(inp["x"], np.float32)
    pos = np.asarray(inp["positions"], np.int32)
    maps = []
    for c in range(NCORES):
        b, q = c // 4, c % 4
        s0 = q * TOK
        if q == 0:
            xh = np.zeros((128, D), np.float32)
            ph = pos[b, 0:128]
        else:
            xh = x[b, s0 - 128:s0]
            ph = pos[b, s0 - 128:s0]
        pp = np.concatenate([ph, pos[b, s0:s0 + TOK]]).reshape(TOK // 128 + 1, 128).T
        maps.append({
            "xs": np.ascontiguousarray(x[b, s0:s0 + TOK]), "xh": np.ascontiguousarray(xh),
            "pos": np.ascontiguousarray(pp.astype(np.int32)),
            "hv": np.full((128, 1), 0.0 if q == 0 else 1.0, np.float32),
            "gm0": _col(inp["mixer_norm"][0], KT), "gf0": _col(inp["ffn_norm"][0], KT),
            "gq": _tile(np.asarray(inp["attn_q_norm"], np.float32)[0]), "gk": _tile(np.asarray(inp["attn_k_norm"], np.float32)[0]),
            "snk": _tile(np.asarray(inp["attn_sinks"], np.float32)[0]),
            "wqkv": np.asarray(inp["attn_w_qkv"], np.float32)[0], "wo": np.asarray(inp["attn_w_o"], np.float32)[0],
            "wg": np.asarray(inp["ffn_w_gate"], np.float32)[0], "wu": np.asarray(inp["ffn_w_up"], np.float32)[0],
            "wd": np.asarray(inp["ffn_w_down"], np.float32)[0],
        })
    return maps


def _b_io(B, full):
    io = {}
    ins = [("x1", [TOK, D], F32), ("x1h", [128, D], F32), ("hv", [128, 1], F32), ("gm1", [128, KT], F32),
           ("cw", [128, 48 * 4], F32), ("cb", [128, 48], F32), ("dtb", [128, 64], F32), ("alog", [128, 64], F32),
           ("win", [D, INW], F32)]
    if full:
        ins += [("dcol", [128, 32], F32), ("ngcol", [128, 32], F32), ("gf1", [128, KT], F32), ("inc", [128, 4], F32),
                ("stS", [4, 128, 4096], F32), ("stD", [4, 128, 64], F32), ("wout", [DIN, D], F32),
                ("wg", [D, DFF], F32), ("wu", [D, DFF], F32), ("wd", [DFF, D], F32)]
    for name, shape, dt in ins:
        io[name] = B.dram(name, shape, dt, "ExternalInput")
    if full:
        io["y"] = B.dram("y", [TOK, D], F32, "ExternalOutput")
    else:
        io["stS_out"] = B.dram("stS_out", [128, 4096], F32, "ExternalOutput")
        io["stD_out"] = B.dram("stD_out", [128, 64], F32, "ExternalOutput")
    return io


def build_B(full):
    B = Builder(nstage=1)
    io = _b_io(B, full)
    outs = phase_B(B, io, full)
    B.S.emit(final_wait_ops=outs)
    return B.nc


def inputs_B(inp, x1, full, states=None):
    x1 = np.asarray(x1, np.float32)
    cw = np.asarray(inp["ssm_conv_w"], np.float32)[0]
    cwl = np.ascontiguousarray(cw.T.reshape(48, 128, 4).transpose(1, 0, 2).reshape(128, 192))
    maps = []
    for c in range(NCORES):
        b, q = c // 4, c % 4
        s0 = q * TOK
        xh = np.zeros((128, D), np.float32) if q == 0 else x1[b, s0 - 128:s0]
        m = {
            "x1": np.ascontiguousarray(x1[b, s0:s0 + TOK]), "x1h": np.ascontiguousarray(xh),
            "hv": np.full((128, 1), 0.0 if q == 0 else 1.0, np.float32),
            "gm1": _col(inp["mixer_norm"][1], KT), "cw": cwl, "cb": _col(np.asarray(inp["ssm_conv_b"], np.float32)[0], 48),
            "dtb": _tile(np.asarray(inp["ssm_dt_bias"], np.float32)[0]), "alog": _tile(np.asarray(inp["ssm_a_log"], np.float32)[0]),
            "win": np.asarray(inp["ssm_w_in"], np.float32)[0],
        }
        if full:
            m.update({
                "dcol": _col(np.repeat(np.asarray(inp["ssm_d"], np.float32)[0], 64), 32),
                "ngcol": _col(np.asarray(inp["ssm_norm"], np.float32)[0], 32),
                "gf1": _col(inp["ffn_norm"][1], KT),
                "inc": np.ascontiguousarray(np.broadcast_to((np.arange(4) < q).astype(np.float32)[None, :], (128, 4))),
                "stS": np.ascontiguousarray(np.stack([states[b * 4 + r][0] for r in range(4)])),
                "stD": np.ascontiguousarray(np.stack([states[b * 4 + r][1] for r in range(4)])),
                "wout": np.asarray(inp["ssm_w_out"], np.float32)[0],
                "wg": np.asarray(inp["ffn_w_gate"], np.float32)[1], "wu": np.asarray(inp["ffn_w_up"], np.float32)[1],
                "wd": np.asarray(inp["ffn_w_down"], np.float32)[1],
            })
        maps.append(m)
    return maps


_NC_CACHE = {}


def _get_nc(name):
    if name not in _NC_CACHE:
        _NC_CACHE[name] = build_A() if name == "A" else build_B(name == "B2")
    return _NC_CACHE[name]


def kernel(**inp):
    cores = list(range(NCORES))
    resA = run_bass_kernel_spmd(_get_nc("A"), inputs_A(inp), core_ids=cores)
    x1 = np.stack([r["x1"] for r in resA.results]).reshape(2, SEQ, D)
    resB1 = run_bass_kernel_spmd(_get_nc("B1"), inputs_B(inp, x1, False), core_ids=cores)
    states = [(r["stS_out"], r["stD_out"]) for r in resB1.results]
    resB2 = run_bass_kernel_spmd(_get_nc("B2"), inputs_B(inp, x1, True, states), core_ids=cores)
    out = np.stack([r["y"] for r in resB2.results]).reshape(2, SEQ, D)
    return out.astype(np.float32)
```

```python
import contextlib
import math
import numpy as np
import concourse.bass as bass
import concourse.mybir as mybir
from concourse.bass_utils import run_bass_kernel_spmd

F32 = mybir.dt.float32
BF16 = mybir.dt.bfloat16
I32 = mybir.dt.int32
AF = mybir.ActivationFunctionType
ALU = mybir.AluOpType
AX = mybir.AxisListType

D = 2048
SEQ = 8192
NCORES = 8
TOK = 2048
MT = 512
NMT = TOK // MT
KT = D // 128
DFF = 5632
FT = DFF // 128
QKVW = 2560
DIN = 4096
INW = 10304
EPS = 1e-6
TWO_PI = 2.0 * math.pi


class Sched:
    ENG = ("pe", "act", "dve", "pool", "sp")

    def __init__(self, nc, n_dma_sems=8):
        self.nc = nc
        self.ops = []
        self.last_writer = {}
        self.readers = {}
        self.n_dma_sems = n_dma_sems

    def op(self, eng, fn, reads=(), writes=(), dma=False, cc=False):
        i = len(self.ops)
        deps = {}
        for k in reads:
            w = self.last_writer.get(k)
            if w is not None:
                deps[w] = True
        for k in writes:
            w = self.last_writer.get(k)
            if w is not None:
                deps.setdefault(w, False)
            for r in self.readers.get(k, ()):
                deps.setdefault(r, False)
        for k in reads:
            self.readers.setdefault(k, []).append(i)
        for k in writes:
            self.last_writer[k] = i
            self.readers[k] = []
        deps.pop(i, None)
        self.ops.append(dict(eng=eng, fn=fn, deps=deps, dma=dma, cc=cc))
        return i

    def emit(self, final_wait_ops=()):
        nc = self.nc
        ops = self.ops
        has_dep = [False] * len(ops)
        for o in ops:
            for d in o["deps"]:
                has_dep[d] = True
        for d in final_wait_ops:
            has_dep[d] = True
        with contextlib.ExitStack() as stack:
            esem = {e: stack.enter_context(nc.semaphore("s_" + e)) for e in ("pe", "act", "dve", "pool")}
            dsem = {e: [stack.enter_context(nc.semaphore("d_%s%d" % (e, j))) for j in range(self.n_dma_sems)]
                    for e in ("sp", "act", "pool")}
            ecount = {e: 0 for e in esem}
            dcount = {e: [0] * self.n_dma_sems for e in dsem}
            dnext = {e: 0 for e in dsem}
            dlast = {e: [None] * self.n_dma_sems for e in dsem}
            done = [None] * len(ops)
            for i, o in enumerate(ops):
                e = o["eng"]
                if o["cc"]:
                    done[i] = (stack.enter_context(nc.semaphore("cc_%d" % i)), 1, "cc")
                elif o["dma"]:
                    j = dnext[e] % self.n_dma_sems
                    dnext[e] += 1
                    if dlast[e][j] is not None:
                        o["deps"][dlast[e][j]] = True
                    dlast[e][j] = i
                    dcount[e][j] += 16
                    done[i] = (dsem[e][j], dcount[e][j], "dma")
                elif has_dep[i]:
                    ecount[e] += 1
                    done[i] = (esem[e], ecount[e], e)
            per_eng = {e: [] for e in self.ENG}
            for i, o in enumerate(ops):
                per_eng[o["eng"]].append(i)
            block = stack.enter_context(nc.Block())

            def make_body(e):
                def body(eng):
                    waited = {}
                    for i in per_eng[e]:
                        o = ops[i]
                        for d in sorted(o["deps"]):
                            sem, val, src = done[d]
                            if src == e and e == "pe":
                                continue
                            key = id(sem)
                            if waited.get(key, 0) >= val:
                                continue
                            waited[key] = val
                            eng.wait_ge(sem, val)
                        ins = o["fn"](eng)
                        if done[i] is not None:
                            sem, val, src = done[i]
                            ins.then_inc(sem, 16 if o["dma"] else 1)
                    if e == "sp":
                        for d in final_wait_ops:
                            sem, val, src = done[d]
                            eng.wait_ge(sem, val)
                return body

            block.tensor(make_body("pe"))
            block.scalar(make_body("act"))
            block.vector(make_body("dve"))
            block.gpsimd(make_body("pool"))
            block.sync(make_body("sp"))


class Builder:
    def __init__(self, nstage=2):
        self.nstage = nstage
        self.pscr = None
        self.phase_key = None
        self.uid = 0
        self.nc = bass.Bass("TRN2", target_bir_lowering=False)
        self.S = Sched(self.nc)
        nc = self.nc
        self.stage = [nc.alloc_sbuf_tensor("wst%d" % i, [128, 2048], F32).ap() for i in range(nstage)]
        self.NRING = 4
        self.ring = [nc.alloc_sbuf_tensor("wrg%d" % i, [128, 2048], BF16).ap() for i in range(self.NRING)]
        self.n_w = 0
        self.n_st = 0
        self.wc = None
        self.ps = [nc.alloc_psum_tensor("ps%d" % i, [128, 512], F32).ap() for i in range(6)]
        self.pt = [nc.alloc_psum_tensor("pt%d" % i, [128, 1024], BF16).ap() for i in range(2)]
        self.n_pt = 0
        self.n_ps = 0
        self.nrot = 6
        self.xres = self.sb("xres", [128, 4, D], F32)
        self.hT = self.sb("hT", [128, KT, MT], BF16)
        self.arena = self.sb("arena", [128, 44 * 512], BF16)
        self.xn = [self.sb("xn%d" % i, [128, D], BF16) for i in range(2)]
        self.junk = self.sb("junk", [128, D], BF16)
        self.stat = self.sb("stat", [128, 8], F32)
        self.sg = [self.sb("sg%d" % i, [128, MT], BF16) for i in range(2)]
        self.ident = self.sb("ident", [128, 128], BF16)
        self.ones = self.sb("ones", [128, 128], BF16)
        self.epsb = self.sb("epsb", [128, 1], F32)
        self.oneb = self.sb("oneb", [128, 1], F32)
        self.I("pool", lambda e: e.memset(self.ones, 1.0), w=["ones"])
        self.I("pool", lambda e: e.memset(self.epsb, EPS), w=["epsb"])
        self.I("pool", lambda e: e.memset(self.oneb, 1.0), w=["oneb"])
        self.I("pool", lambda e: e.affine_select(out=self.ident, in_=self.ones, pattern=[[-1, 128]],
                                                 compare_op=ALU.is_equal, fill=0.0, base=0, channel_multiplier=1),
               r=["ones"], w=["ident"])
        self.I("dve", lambda e: e.memset(self.stat[:, 4:5], 0.0), w=["arena_own"])

    def I(self, eng, fn, r=(), w=(), dma=False, cc=False):
        if self.phase_key is not None:
            r = list(r) + [self.phase_key]
        return self.S.op(eng, fn, r, w, dma, cc)

    def sb(self, name, shape, dt):
        if self.pscr is None:
            return self.nc.alloc_sbuf_tensor("sb_" + name, shape, dt).ap()
        esz = 2 if dt == BF16 else 4
        n = 1
        for d_ in shape[1:]:
            n *= d_
        nbytes = ((n * esz + 31) // 32) * 32
        o = self.pscr_off
        self.pscr_off += nbytes
        assert self.pscr_off <= self.pscr_bytes, (name, self.pscr_off, self.pscr_bytes)
        base = self.pscr if dt == BF16 else self.pscr.bitcast(dt)
        v = base[0:shape[0], o // esz:o // esz + n]
        if len(shape) == 2:
            return v
        names = " ".join("d%d" % i for i in range(len(shape) - 1))
        kw = {"d%d" % i: shape[i + 1] for i in range(len(shape) - 2)}
        return v.rearrange("p (%s) -> p %s" % (names, names), **kw)

    def enable_phase_scratch(self):
        nbytes = ((self.nc.sbuf_bytes_remaining - 1024) // 64) * 64
        self.pscr = self.nc.alloc_sbuf_tensor("sb_pscr", [128, nbytes // 2], BF16).ap()
        self.pscr_bytes = nbytes
        self.pscr_off = 0

    def begin_phase(self, name):
        self.phase_key = None
        self.I("dve", lambda e: e.memset(self.stat[:, 6:7], 0.0), w=["phase_own", "arena_own"])
        self.phase_key = "phase_own"
        self.pscr_off = 0
        self.uid += 1

    def dram(self, name, shape, dt, kind="Internal"):
        return self.nc.dram_tensor(name, shape, dt, kind=kind).ap()

    def wload(self, W, r0, KC, c0, NC):
        n = self.n_w
        self.n_w += 1
        rg, rk = self.ring[n % self.NRING], ("wrg", n % self.NRING)
        assert KC * NC <= 2048
        sz = KC * NC
        key = (W.tensor.name, r0, KC, c0, NC)
        use_c = self.wc is not None and W.tensor.name in self.cache_names
        if use_c and key in self.wcache:
            off = self.wcache[key]
            self.I("sp", lambda e: e.dma_start(out=rg[:, 0:sz], in_=self.wc[:, off:off + sz]), r=[("wc", off)], w=[rk], dma=True)
        else:
            m = self.n_st
            self.n_st += 1
            st, sk = self.stage[m % self.nstage], ("wst", m % self.nstage)
            src = W[r0:r0 + KC * 128, c0:c0 + NC].rearrange("(k p) n -> p k n", p=128)
            stv = st[:, 0:sz].rearrange("p (k n) -> p k n", k=KC)
            self.I("sp", lambda e: e.dma_start(out=stv, in_=src), w=[sk], dma=True)
            self.I("pool", lambda e: e.tensor_copy(out=rg[:, 0:sz], in_=st[:, 0:sz]), r=[sk], w=[rk])
            if use_c:
                off = self.wc_off
                self.wc_off += sz
                assert self.wc_off <= self.wc.shape[1]
                self.wcache[key] = off
                self.I("sp", lambda e: e.dma_start(out=self.wc[:, off:off + sz], in_=rg[:, 0:sz]), r=[rk], w=[("wc", off)], dma=True)
        return rg[:, 0:sz].rearrange("p (k n) -> p k n", k=KC), rk

    def enable_weight_cache(self, ncols, names=()):
        self.cache_names = set(names)
        self.wc = self.nc.dram_tensor("wcache", [128, ncols], BF16).ap()
        self.wc_off = 0
        self.wcache = {}

    def ptbank(self):
        i = self.n_pt % 2
        self.n_pt += 1
        return self.pt[i], ("pt", i)

    def psbank(self):
        i = self.n_ps % self.nrot
        self.n_ps += 1
        return i

    def load_cols(self, name, src, ncol, dt=F32):
        t = self.sb(name, [128, ncol], dt)
        self.I("sp", lambda e: e.dma_start(out=t, in_=src), w=[name], dma=True)
        return t

    def arena_barrier(self):
        self.I("dve", lambda e: e.memset(self.stat[:, 4:5], 0.0), w=["arena_own"])

    def norm_T(self, gcol, gkey, ntt=4, src=None, srckeys=None, dst=None, dstkey="hT", xr=()):
        xr = list(xr)
        src = self.xres if src is None else src
        dst = self.hT if dst is None else dst
        for tt in range(ntt):
            xk = ("xres", tt) if srckeys is None else srckeys[tt]
            xin = src[:, tt, :]
            xn, xnk = self.xn[tt % 2], ("xn", tt % 2)
            ss = self.stat[:, (tt % 2) * 2:(tt % 2) * 2 + 1]
            rs = self.stat[:, (tt % 2) * 2 + 1:(tt % 2) * 2 + 2]
            ssk, rsk = ("nss", tt % 2), ("nrs", tt % 2)
            self.I("act", lambda e, xin=xin, ss=ss: e.activation(out=self.junk, in_=xin, func=AF.Square, accum_out=ss),
                   r=[xk] + xr, w=["junk", ssk])
            self.I("act", lambda e, ss=ss, rs=rs: e.activation(out=rs, in_=ss, func=AF.Sqrt, bias=self.epsb, scale=1.0 / D),
                   r=[ssk, "epsb"], w=[rsk])
            self.I("dve", lambda e, rs=rs: e.reciprocal(out=rs, in_=rs), r=[rsk], w=[rsk])
            self.I("act", lambda e, xin=xin, xn=xn, rs=rs: e.activation(out=xn, in_=xin, func=AF.Copy, scale=rs),
                   r=[xk, rsk] + xr, w=[xnk])
            for half in range(2):
                pt, ptk = self.ptbank()

                def tr(e, pt=pt, xn=xn, half=half):
                    r = None
                    for j in range(8):
                        k = half * 8 + j
                        r = e.transpose(out=pt[:, j * 128:(j + 1) * 128], in_=xn[:, k * 128:(k + 1) * 128], identity=self.ident)
                    return r
                self.I("pe", tr, r=[xnk, "ident"], w=[ptk])
                self.I("dve", lambda e, pt=pt, half=half, tt=tt: e.tensor_tensor(
                    out=dst[:, half * 8:(half + 1) * 8, tt * 128:(tt + 1) * 128],
                    in0=pt.rearrange("p (k t) -> p k t", k=8),
                    in1=gcol[:, half * 8:(half + 1) * 8].unsqueeze(2).to_broadcast([128, 8, 128]), op=ALU.mult),
                    r=[ptk, gkey] + xr, w=[(dstkey, tt)])

    def proj_out(self, actT, actkeys, nk, W, kc=4):
        nch = (nk + kc - 1) // kc
        for cb in range(4):
            banks = [self.psbank() for tt in range(4)]
            for ci in range(nch):
                k0 = ci * kc
                kk_n = min(kc, nk - k0)
                wv, wk = self.wload(W, k0 * 128, kk_n, cb * 512, 512)
                for tt in range(4):
                    def mm(e, wv=wv, tt=tt, k0=k0, kk_n=kk_n, b=banks[tt]):
                        r = None
                        for kk in range(kk_n):
                            k = k0 + kk
                            r = e.matmul(self.ps[b], lhsT=actT[:, k, tt * 128:(tt + 1) * 128], rhs=wv[:, kk, :],
                                         start=(k == 0), stop=(k == nk - 1))
                        return r
                    self.I("pe", mm, r=[wk] + list(actkeys), w=[("ps", banks[tt])])
            for tt in range(4):
                xv = self.xres[:, tt, cb * 512:(cb + 1) * 512]
                self.I("dve", lambda e, xv=xv, b=banks[tt]: e.tensor_tensor(out=xv, in0=self.ps[b], in1=xv, op=ALU.add),
                       r=[("ps", banks[tt]), ("xres", tt)], w=[("xres", tt)])

    def ffn(self, gcol, gkey, Wg, Wu, Wd):
        self.norm_T(gcol, gkey)
        self.arena_barrier()
        hid = self.arena[:, 0:FT * MT].rearrange("p (f t) -> p f t", f=FT)
        hkeys = [("hT", tt) for tt in range(4)]
        for c in range(FT // 2):
            bg = [self.psbank(), self.psbank()]
            bu = [self.psbank(), self.psbank()]
            for hk_ in range(2):
                gv, gk = self.wload(Wg, hk_ * 8 * 128, 8, c * 256, 256)
                uv, uk = self.wload(Wu, hk_ * 8 * 128, 8, c * 256, 256)
                for j in range(2):
                    def mmw(e, wv=None, j=j, b=None, hk_=hk_):
                        r = None
                        for kk in range(8):
                            k = hk_ * 8 + kk
                            r = e.matmul(self.ps[b], lhsT=wv[:, kk, j * 128:(j + 1) * 128], rhs=self.hT[:, k, :],
                                         start=(k == 0), stop=(k == KT - 1))
                        return r
                    self.I("pe", lambda e, j=j, b=bg[j], wv=gv, f_=mmw: f_(e, wv, j, b), r=[gk] + hkeys, w=[("ps", bg[j])])
                    self.I("pe", lambda e, j=j, b=bu[j], wv=uv, f_=mmw: f_(e, wv, j, b), r=[uk] + hkeys, w=[("ps", bu[j])])
            for j in range(2):
                f = c * 2 + j
                sg = self.sg[f % 2]
                self.I("act", lambda e, sg=sg, b=bg[j]: e.activation(out=sg, in_=self.ps[b], func=AF.Silu),
                       r=[("ps", bg[j])], w=[("sg", f % 2)])
                self.I("dve", lambda e, sg=sg, b=bu[j], f=f: e.tensor_tensor(out=hid[:, f, :], in0=self.ps[b], in1=sg, op=ALU.mult),
                       r=[("ps", bu[j]), ("sg", f % 2), "arena_own"], w=[("hid", f)])
        hk = [("hid", f) for f in range(FT)] + ["arena_own"]
        self.proj_out(hid, hk, FT, Wd, kc=4)

    def load_rows(self, src, mt, rkey=None):
        for tt in range(4):
            r0 = mt * MT + tt * 128
            self.I("sp", lambda e, tt=tt, r0=r0: e.dma_start(out=self.xres[:, tt, :], in_=src[r0:r0 + 128, :]),
                   r=([rkey] if rkey else []), w=[("xres", tt)], dma=True)

    def store_rows(self, dst, mt, wkey=None):
        outs = []
        for tt in range(4):
            r0 = mt * MT + tt * 128
            outs.append(self.I("sp", lambda e, tt=tt, r0=r0: e.dma_start(out=dst[r0:r0 + 128, :], in_=self.xres[:, tt, :]),
                               r=[("xres", tt)], w=([wkey] if wkey else []), dma=True))
        return outs


def phase_A(B, io):
    I = B.I
    x_in, xh_in = io["xs"], io["xh"]
    Wqkv, Wo = io["wqkv"], io["wo"]
    x_out = io["x1"]

    gm = B.load_cols("gm0", io["gm0"], KT)
    gf = B.load_cols("gf0", io["gf0"], KT)
    gq = B.load_cols("gq", io["gq"], 64)
    gk = B.load_cols("gk", io["gk"], 64)
    snk = B.load_cols("snk", io["snk"], 32)
    hv = B.load_cols("hv", io["hv"], 1)
    NT1 = TOK // 128 + 1
    posi = B.load_cols("posi", io["pos"], NT1, I32)

    posf = B.sb("posf", [128, NT1], F32)
    invf = B.sb("invf", [128, 32], F32)
    AO = "arena_own"
    a32 = B.arena.bitcast(F32)
    def atmp(i):
        return a32[:, 8192 + i * NT1 * 32:8192 + (i + 1) * NT1 * 32].rearrange("p (t c) -> p t c", t=NT1)
    ang, ang2, kf, msk = atmp(0), atmp(1), atmp(2), atmp(3)
    kq = atmp(4).bitcast(I32)
    sin_t = B.sb("sin_t", [128, NT1, 32], F32)
    cos_t = B.sb("cos_t", [128, NT1, 32], F32)
    I("dve", lambda e: e.tensor_copy(out=posf, in_=posi), r=["posi"], w=["posf"])

    def mk_invf(e):
        r = None
        for i in range(32):
            r = e.memset(invf[:, i:i + 1], float(np.float32(10000.0) ** np.float32(-(2.0 * i) / 64.0)))
        return r
    I("pool", mk_invf, w=["invf"])
    C1 = 6.28125
    C2 = TWO_PI - C1
    V = lambda fn, r, w: I("dve", fn, list(r) + [AO], w)
    V(lambda e: e.tensor_tensor(out=ang, in0=posf.unsqueeze(2).to_broadcast([128, NT1, 32]),
                                in1=invf.unsqueeze(1).to_broadcast([128, NT1, 32]), op=ALU.mult), ["posf", "invf"], ["ang"])
    V(lambda e: e.tensor_scalar(out=kq, in0=ang, scalar1=1.0 / TWO_PI, scalar2=None, op0=ALU.mult), ["ang"], ["kq"])
    V(lambda e: e.tensor_copy(out=kf, in_=kq), ["kq"], ["kf"])
    V(lambda e: e.scalar_tensor_tensor(out=ang2, in0=kf, scalar=-C1, in1=ang, op0=ALU.mult, op1=ALU.add), ["kf", "ang"], ["ang2"])
    V(lambda e: e.scalar_tensor_tensor(out=ang, in0=kf, scalar=-C2, in1=ang2, op0=ALU.mult, op1=ALU.add), ["kf", "ang2"], ["ang"])
    V(lambda e: e.tensor_scalar(out=msk, in0=ang, scalar1=math.pi, scalar2=-TWO_PI, op0=ALU.is_gt, op1=ALU.mult), ["ang"], ["msk"])
    V(lambda e: e.tensor_tensor(out=ang2, in0=ang, in1=msk, op=ALU.add), ["ang", "msk"], ["ang2"])
    V(lambda e: e.tensor_scalar(out=msk, in0=ang2, scalar1=-math.pi, scalar2=TWO_PI, op0=ALU.is_lt, op1=ALU.mult), ["ang2"], ["msk"])
    V(lambda e: e.tensor_tensor(out=ang, in0=ang2, in1=msk, op=ALU.add), ["ang2", "msk"], ["ang"])
    V(lambda e: e.tensor_scalar(out=kf, in0=ang, scalar1=math.pi / 2, scalar2=None, op0=ALU.add), ["ang"], ["kf"])
    V(lambda e: e.tensor_scalar(out=msk, in0=kf, scalar1=math.pi, scalar2=-TWO_PI, op0=ALU.is_gt, op1=ALU.mult), ["kf"], ["msk"])
    V(lambda e: e.tensor_tensor(out=ang2, in0=kf, in1=msk, op=ALU.add), ["kf", "msk"], ["ang2"])
    PI_S = 3.1415925
    V(lambda e: e.tensor_scalar(out=kf, in0=ang, scalar1=PI_S, scalar2=-PI_S, op0=ALU.min, op1=ALU.max), ["ang"], ["kf"])
    V(lambda e: e.tensor_scalar(out=msk, in0=ang2, scalar1=PI_S, scalar2=-PI_S, op0=ALU.min, op1=ALU.max), ["ang2"], ["msk"])
    I("act", lambda e: e.activation(out=sin_t, in_=kf, func=AF.Sin), r=["kf", AO], w=["sin_t"])
    I("act", lambda e: e.activation(out=cos_t, in_=msk, func=AF.Sin), r=["msk", AO], w=["cos_t"])
    V = lambda fn, r, w: I("dve", fn, r, w)

    negM = B.sb("negM", [128, 1], F32)
    mtmp = B.sb("mtmp", [128, 4], F32)
    sinkexp = B.sb("sinkexp", [128, 32], F32)
    V(lambda e: e.tensor_reduce(out=mtmp[:, 0:1], in_=gq, axis=AX.X, op=ALU.max, apply_absolute_value=True), ["gq"], ["mt0"])
    V(lambda e: e.tensor_reduce(out=mtmp[:, 1:2], in_=gk, axis=AX.X, op=ALU.max, apply_absolute_value=True), ["gk"], ["mt1"])
    V(lambda e: e.tensor_tensor(out=mtmp[:, 2:3], in0=mtmp[:, 0:1], in1=mtmp[:, 1:2], op=ALU.mult), ["mt0", "mt1"], ["mt2"])
    V(lambda e: e.tensor_scalar(out=negM, in0=mtmp[:, 2:3], scalar1=-8.0, scalar2=None, op0=ALU.mult), ["mt2"], ["negM"])
    I("act", lambda e: e.activation(out=sinkexp, in_=snk, func=AF.Exp, bias=negM, scale=1.0), r=["snk", "negM"], w=["sinkexp"])

    onesw = B.sb("onesw", [128, 512], BF16)
    maskP = B.sb("maskP", [128, 512], BF16)
    maskO = B.sb("maskO", [128, 512], BF16)
    maskP0 = B.sb("maskP0", [128, 512], BF16)
    I("pool", lambda e: e.memset(onesw, 1.0), w=["onesw"])
    I("pool", lambda e: e.affine_select(out=maskP, in_=onesw, pattern=[[0, 4], [-1, 128]], compare_op=ALU.is_gt,
                                        fill=0.0, base=0, channel_multiplier=1), r=["onesw"], w=["maskP"])
    I("pool", lambda e: e.affine_select(out=maskO, in_=onesw, pattern=[[0, 4], [1, 128]], compare_op=ALU.is_ge,
                                        fill=0.0, base=0, channel_multiplier=-1), r=["onesw"], w=["maskO"])
    V(lambda e: e.tensor_scalar(out=maskP0, in0=maskP, scalar1=hv[:, 0:1], scalar2=None, op0=ALU.mult), ["maskP", "hv"], ["maskP0"])

    sq = B.sb("sq", [128, 512], F32)
    hss = B.sb("hss", [128, 8], F32)
    hrs = B.sb("hrs", [128, 8], F32)
    qn = B.sb("qn", [128, 512], F32)
    qn2 = sq
    rtmp = B.sb("rtmp", [128, 4, 8, 32], F32)
    kdup = B.sb("kdup", [128, 4, 2, 64], BF16)
    qT = B.sb("qT", [128, 16, 128], BF16)
    kT5 = B.sb("kT5", [128, 5, 4, 128], BF16)
    V5 = B.sb("V5", [128, 5, 256], BF16)
    PT = [B.sb("PT%d" % i, [128, 512], BF16) for i in range(4)]
    den = B.sb("den", [128, 512], F32)
    rden = den
    hTh = B.arena[:, 4096:4096 + KT * 128].rearrange("p (k t) -> p k t", k=KT)
    xhalo = a32[:, 0:D].rearrange("p (o c) -> p o c", o=1)
    attnT = B.arena[:, 0:KT * MT].rearrange("p (k t) -> p k t", k=KT)
    qrot4 = B.arena[:, KT * MT:KT * MT + 4 * 2048].rearrange("p (t c) -> p t c", t=4)

    def qk_post(b, nh, gt, gvec, gkey, out_ap, outkey, extra_r=()):
        pv = B.ps[b][:, 0:nh * 64]
        p3 = pv.rearrange("p (h d) -> p h d", h=nh)
        I("act", lambda e: e.activation(out=sq[:, 0:nh * 64], in_=pv, func=AF.Square), r=[("ps", b)], w=["sq"])
        V(lambda e: e.tensor_reduce(out=hss[:, 0:nh], in_=sq[:, 0:nh * 64].rearrange("p (h d) -> p h d", h=nh),
                                    axis=AX.X, op=ALU.add), ["sq"], ["hss"])
        I("act", lambda e: e.activation(out=hrs[:, 0:nh], in_=hss[:, 0:nh], func=AF.Sqrt, bias=B.epsb, scale=1.0 / 64),
          r=["hss", "epsb"], w=["hrs"])
        V(lambda e: e.reciprocal(out=hss[:, 0:nh], in_=hrs[:, 0:nh]), ["hrs"], ["hss2"])
        q3 = qn[:, 0:nh * 64].rearrange("p (h d) -> p h d", h=nh)
        q32 = qn2[:, 0:nh * 64].rearrange("p (h d) -> p h d", h=nh)
        cs = cos_t[:, gt, :].unsqueeze(1).to_broadcast([128, nh, 32])
        sn = sin_t[:, gt, :].unsqueeze(1).to_broadcast([128, nh, 32])
        t1, t2 = q32[:, :, 0:32], q32[:, :, 32:64]
        ra, rb, rc, rd = (rtmp[:, i, 0:nh, :] for i in range(4))
        V(lambda e: e.tensor_tensor(out=q3, in0=p3, in1=hss[:, 0:nh].unsqueeze(2).to_broadcast([128, nh, 64]), op=ALU.mult),
          [("ps", b), "hss2"], ["qn"])
        V(lambda e: e.tensor_tensor(out=q32, in0=q3, in1=gvec.unsqueeze(1).to_broadcast([128, nh, 64]), op=ALU.mult),
          ["qn", gkey], ["sq"])
        V(lambda e: e.tensor_tensor(out=ra, in0=t1, in1=cs, op=ALU.mult), ["sq", "cos_t"], ["ra"])
        V(lambda e: e.tensor_tensor(out=rb, in0=t2, in1=sn, op=ALU.mult), ["sq", "sin_t"], ["rb"])
        V(lambda e: e.tensor_tensor(out=rc, in0=t2, in1=cs, op=ALU.mult), ["sq", "cos_t"], ["rc"])
        V(lambda e: e.tensor_tensor(out=rd, in0=t1, in1=sn, op=ALU.mult), ["sq", "sin_t"], ["rd"])
        V(lambda e: e.tensor_tensor(out=out_ap[:, :, 0:32], in0=ra, in1=rb, op=ALU.subtract), ["ra", "rb"] + list(extra_r), [outkey + ("a",)])
        V(lambda e: e.tensor_tensor(out=out_ap[:, :, 32:64], in0=rc, in1=rd, op=ALU.add), ["rc", "rd"] + list(extra_r), [outkey + ("b",)])

    def kv_finish(b, gt, slot):
        I("act", lambda e: e.activation(out=V5[:, slot, :], in_=B.ps[b][:, 256:512], func=AF.Copy), r=[("ps", b)], w=[("V", slot)])
        qk_post(b, 4, gt, gk, "gk", kdup[:, :, 0, :], ("kdup0",))
        I("act", lambda e: e.activation(out=kdup[:, :, 1, :], in_=kdup[:, :, 0, :], func=AF.Copy),
          r=[("kdup0", "a"), ("kdup0", "b")], w=["kdup1"])
        pt, ptk = B.ptbank()

        def tr(e):
            r = None
            for kv in range(4):
                r = e.transpose(out=pt[:, kv * 128:(kv + 1) * 128], in_=kdup[:, kv, :, :].rearrange("p a d -> p (a d)"),
                                identity=B.ident)
            return r
        I("pe", tr, r=[("kdup0", "a"), ("kdup0", "b"), "kdup1", "ident"], w=[ptk])
        I("act", lambda e: e.activation(out=kT5[:, slot, :, :].rearrange("p k t -> p (k t)"), in_=pt[:, 0:512], func=AF.Copy),
          r=[ptk], w=[("kT", slot)])

    I("sp", lambda e: e.dma_start(out=xhalo[:, 0, :], in_=xh_in), r=[AO], w=[("xhalo", 0)], dma=True)
    B.norm_T(gm, "gm0", ntt=1, src=xhalo, srckeys=[("xhalo", 0)], dst=hTh, dstkey="hTh", xr=[AO])
    bh = B.psbank()
    for ci in range(4):
        wv, wk = B.wload(Wqkv, ci * 4 * 128, 4, 2048, 512)

        def mm(e, wv=wv, ci=ci):
            r = None
            for kk in range(4):
                k = ci * 4 + kk
                r = e.matmul(B.ps[bh], lhsT=hTh[:, k, :], rhs=wv[:, kk, :], start=(k == 0), stop=(k == KT - 1))
            return r
        I("pe", mm, r=[wk, ("hTh", 0), AO], w=[("ps", bh)])
    kv_finish(bh, 0, 0)

    outs = []
    for mt in range(NMT):
        B.load_rows(x_in, mt)
        B.norm_T(gm, "gm0")
        B.arena_barrier()
        for cb in (4, 0, 1, 2, 3):
            banks = [B.psbank() for tt in range(4)]
            for ci in range(4):
                wv, wk = B.wload(Wqkv, ci * 4 * 128, 4, cb * 512, 512)
                for tt in range(4):
                    def mm(e, wv=wv, ci=ci, tt=tt, b=banks[tt]):
                        r = None
                        for kk in range(4):
                            k = ci * 4 + kk
                            r = e.matmul(B.ps[b], lhsT=B.hT[:, k, tt * 128:(tt + 1) * 128], rhs=wv[:, kk, :],
                                         start=(k == 0), stop=(k == KT - 1))
                        return r
                    I("pe", mm, r=[wk, ("hT", tt)], w=[("ps", banks[tt])])
            for tt in range(4):
                gt = mt * 4 + tt + 1
                if cb == 4:
                    kv_finish(banks[tt], gt, tt + 1)
                else:
                    qk_post(banks[tt], 8, gt, gq, "gq",
                            qrot4[:, tt, cb * 512:(cb + 1) * 512].rearrange("p (h d) -> p h d", h=8),
                            ("qrot", tt, cb), extra_r=["arena_own"])
        for tt in range(4):
            qk_keys = [("qrot", tt, cb, ab) for cb in range(4) for ab in ("a", "b")]
            for half in range(2):
                pt, ptk = B.ptbank()

                def tr(e, pt=pt, half=half, tt=tt):
                    r = None
                    for j in range(8):
                        blk = half * 8 + j
                        r = e.transpose(out=pt[:, j * 128:(j + 1) * 128], in_=qrot4[:, tt, blk * 128:(blk + 1) * 128], identity=B.ident)
                    return r
                I("pe", tr, r=qk_keys + ["ident", "arena_own"], w=[ptk])
                I("act", lambda e, pt=pt, half=half: e.activation(
                    out=qT[:, half * 8:(half + 1) * 8, :].rearrange("p k t -> p (k t)"), in_=pt, func=AF.Copy),
                  r=[ptk], w=[("qT", half)])
            for kv in range(4):
                sb_ = [B.psbank() for _ in range(4)]
                bo, bd = B.psbank(), B.psbank()
                idx = 0
                for blk, slot in ((0, tt), (1, tt + 1)):
                    for par in range(2):
                        b = sb_[idx]
                        I("pe", lambda e, b=b, par=par, slot=slot, kv=kv: e.matmul(
                            B.ps[b], lhsT=kT5[par * 64:(par + 1) * 64, slot, kv, :],
                            rhs=qT[par * 64:(par + 1) * 64, 4 * kv:4 * kv + 4, :].rearrange("p k t -> p (k t)"),
                            start=True, stop=True),
                          r=[("kT", slot), ("qT", kv // 2)], w=[("ps", b)])
                        I("act", lambda e, b=b, idx=idx: e.activation(out=PT[idx], in_=B.ps[b], func=AF.Exp, bias=negM, scale=0.125),
                          r=[("ps", b), "negM"], w=[("PTr", idx)])
                        if blk == 0:
                            mk, mkk = (maskP0, "maskP0") if (mt == 0 and tt == 0) else (maskP, "maskP")
                        else:
                            mk, mkk = maskO, "maskO"
                        V(lambda e, idx=idx, mk=mk: e.tensor_tensor(out=PT[idx], in0=PT[idx], in1=mk, op=ALU.mult),
                          [("PTr", idx), mkk], [("PT", idx)])
                        idx += 1
                for par in range(2):
                    for blk, slot in ((0, tt), (1, tt + 1)):
                        idx = blk * 2 + par
                        I("pe", lambda e, par=par, slot=slot, idx=idx, blk=blk, kv=kv, bo=bo: e.matmul(
                            B.ps[bo][par * 64:(par + 1) * 64, :], lhsT=V5[:, slot, kv * 64:(kv + 1) * 64], rhs=PT[idx],
                            start=(blk == 0), stop=(blk == 1)),
                          r=[("V", slot), ("PT", idx)], w=[("ps", bo)])
                    for blk, slot in ((0, tt), (1, tt + 1)):
                        idx = blk * 2 + par
                        I("pe", lambda e, par=par, idx=idx, blk=blk, bd=bd: e.matmul(
                            B.ps[bd][par * 64:(par + 1) * 64, :], lhsT=B.ones[:, 0:64], rhs=PT[idx],
                            start=(blk == 0), stop=(blk == 1)),
                          r=["ones", ("PT", idx)], w=[("ps", bd)])
                for par in range(2):
                    for pair in range(4):
                        h = kv * 8 + 2 * pair + par
                        V(lambda e, par=par, pair=pair, h=h, bd=bd: e.tensor_scalar(
                            out=den[par * 64:(par + 1) * 64, pair * 128:(pair + 1) * 128],
                            in0=B.ps[bd][par * 64:(par + 1) * 64, pair * 128:(pair + 1) * 128],
                            scalar1=sinkexp[par * 64:(par + 1) * 64, h:h + 1], scalar2=None, op0=ALU.add),
                          [("ps", bd), "sinkexp"], [("den", par, pair)])
                dk = [("den", par, pair) for par in range(2) for pair in range(4)]
                V(lambda e: e.reciprocal(out=rden, in_=den), dk, ["rden"] + dk)
                V(lambda e, kv=kv, tt=tt, bo=bo: e.tensor_tensor(
                    out=attnT[:, 4 * kv:4 * kv + 4, tt * 128:(tt + 1) * 128],
                    in0=B.ps[bo].rearrange("p (k t) -> p k t", k=4), in1=rden.rearrange("p (k t) -> p k t", k=4), op=ALU.mult),
                  [("ps", bo), "rden", "arena_own"], [("attnT", tt, kv)])
        I("act", lambda e: e.activation(out=kT5[:, 0, :, :], in_=kT5[:, 4, :, :], func=AF.Copy), r=[("kT", 4)], w=[("kT", 0)])
        I("act", lambda e: e.activation(out=V5[:, 0, :], in_=V5[:, 4, :], func=AF.Copy), r=[("V", 4)], w=[("V", 0)])
        ak = [("attnT", tt, kv) for tt in range(4) for kv in range(4)] + ["arena_own"]
        B.proj_out(attnT, ak, KT, Wo)
        B.ffn(gf, "gf0", io["wg"], io["wu"], io["wd"])
        outs += B.store_rows(x_out, mt, wkey=("x1d" if "ag1_in" in io else None))
    if "ag1_in" in io:
        I("sp", lambda e: e.dma_start(out=io["ag1_in"], in_=B.xres[:, 3, :]), r=[("xres", 3)], w=["ag1_in"], dma=True)
        I("pool", lambda e: e.collective_compute("AllGather", ALU.bypass, replica_groups=[list(range(NCORES))],
                                                 ins=[io["ag1_in"].opt()], outs=[io["ag1_out"].opt()]),
          r=["ag1_in"], w=["ag1_out"], cc=True)
        outs = []
    return outs


def phase_B(B, io, full):
    I = B.I
    AO = "arena_own"
    B.nrot = 3
    V = lambda fn, r, w: I("dve", fn, r, w)
    A = lambda fn, r, w: I("act", fn, r, w)
    x_in, xh_in, Win = io["x1"], io.get("x1h"), io["win"]
    x1key = "x1d" if "ag1_out" in io else None
    gm = B.load_cols("gm1", io["gm1"], KT)
    hv = B.load_cols("hvb", io["hv"], 1)
    cw = B.load_cols("cw", io["cw"], 48 * 4)
    cbias = B.load_cols("cbias", io["cb"], 48)
    dtb = B.load_cols("dtb", io["dtb"], 64)
    alog = B.load_cols("alog", io["alog"], 64)
    if full:
        dcol = B.load_cols("dcol", io["dcol"], 32)
        ngcol = B.load_cols("ngcol", io["ngcol"], 32)
        gf = B.load_cols("gf1", io["gf1"], KT)
        inc = B.load_cols("inc", io["inc"], io["inc"].shape[1])

    onesf = B.sb("onesf", [128, 256], F32)
    trif = B.sb("trif", [128, 128], F32)
    tri0 = B.sb("tri0", [128, 256], F32)
    tri1 = B.sb("tri1", [128, 256], F32)
    a_t = B.sb("a_t", [128, 64], F32)
    I("pool", lambda e: e.memset(onesf, 1.0), w=["onesf"])
    I("pool", lambda e: e.affine_select(out=trif, in_=onesf[:, 0:128], pattern=[[1, 128]], compare_op=ALU.is_ge, fill=0.0,
                                        base=0, channel_multiplier=-1), r=["onesf"], w=["trif"])
    I("pool", lambda e: e.affine_select(out=tri0, in_=onesf, pattern=[[1, 256]], compare_op=ALU.is_ge, fill=0.0,
                                        base=0, channel_multiplier=-1), r=["onesf"], w=["tri0"])
    I("pool", lambda e: e.affine_select(out=tri1, in_=onesf, pattern=[[1, 256]], compare_op=ALU.is_ge, fill=0.0,
                                        base=-128, channel_multiplier=-1), r=["onesf"], w=["tri1"])
    A(lambda e: e.activation(out=a_t, in_=alog, func=AF.Exp), ["alog"], ["a_e"])
    V(lambda e: e.tensor_scalar(out=a_t, in0=a_t, scalar1=-1.0, scalar2=None, op0=ALU.mult), ["a_e"], ["a_t"])

    st = B.sb("st", [128, 8, 512], F32)
    stbf = B.sb("stbf", [128, 512], BF16)
    carry = B.sb("carry", [128, 48, 3], F32)
    dtraw = B.sb("dtraw", [128, 4, 64], F32)
    dtv = B.sb("dtv", [128, 4, 64], F32)
    dtmp = B.sb("dtmp", [128, 4, 64], F32)
    adt = B.sb("adt", [128, 4, 64], F32)
    acum = B.sb("acum", [128, 4, 64], F32)
    te = B.sb("te", [128, 4, 64], F32)
    tot = B.sb("tot", [128, 2, 64], F32)
    dec = B.sb("dec", [128, 2, 64], F32)
    segtot = B.sb("segtot", [128, 64], F32)
    ubuf = B.sb("ubuf", [128, 515], F32)
    acc = [B.sb("acc%d" % i, [128, 512], F32) for i in range(2)]
    BT = B.sb("BT", [128, 512], BF16)
    Btm = B.sb("Btm", [128, 4, 128], BF16)
    xdt = B.sb("xdt", [128, 4, 512], BF16)
    xdte = B.sb("xdte", [128, 2, 512], BF16)
    hTh = B.sb("hThb", [128, KT, 4], BF16)
    xT_g = B.arena[:, 16384:16384 + 2048].rearrange("p (j t) -> p j t", j=4)
    if full:
        CT = B.sb("CT", [128, 512], BF16)
        CBm = [B.sb("CBm%d" % i, [128, 256], BF16) for i in range(2)]
        adtrep = [B.sb("adtrep%d" % i, [128, 128], F32) for i in range(2)]
        darg = [B.sb("darg%d" % i, [128, 256], F32) for i in range(2)]
        Mt = [B.sb("Mt%d" % i, [128, 256], BF16) for i in range(2)]
        eoff = B.sb("eoff", [128, 256], F32)
        roff = B.sb("roff", [128, 256], BF16)
        sqn = B.sb("sqn", [128, 4, 256], BF16)
        rstd = B.sb("rstd", [128, 256], F32)
        ynT = B.arena[:, 0:32 * 512].rearrange("p (k t) -> p k t", k=32)
        zs = B.arena[:, 18432:18432 + 2048].rearrange("p (j t) -> p j t", j=4)
        yg = B.arena.bitcast(F32)[:, 10240:10240 + 1024].rearrange("p (j t) -> p j t", j=4)
    I("pool", lambda e: e.memset(segtot, 0.0), w=["segtot"])

    if not full:
        I("pool", lambda e: e.memset(st.rearrange("p g c -> p (g c)"), 0.0), w=[("st", g) for g in range(8)])
    else:
        sS = B.arena.bitcast(F32)[:, 0:4096]
        sD = B.sb("sD", [128, 64], F32)
        fr = B.sb("fr", [128, 64], F32)
        I("pool", lambda e: e.memset(st.rearrange("p g c -> p (g c)"), 0.0), w=["stall"])
        fusedst = "ag2_out" in io
        for r in range(8 if fusedst else 3):
            if fusedst:
                I("sp", lambda e, r=r: e.dma_start(out=sS, in_=io["ag2_out"][r * 128:(r + 1) * 128, 0:4096]), r=[AO, "ag2_out"], w=["sS"], dma=True)
                I("sp", lambda e, r=r: e.dma_start(out=sD, in_=io["ag2_out"][r * 128:(r + 1) * 128, 4096:4160]), r=["ag2_out"], w=["sD"], dma=True)
            else:
                I("sp", lambda e, r=r: e.dma_start(out=sS, in_=io["stS"][r]), r=[AO], w=["sS"], dma=True)
                I("sp", lambda e, r=r: e.dma_start(out=sD, in_=io["stD"][r]), w=["sD"], dma=True)
            V(lambda e, r=r: e.tensor_scalar(out=fr, in0=sD, scalar1=-1.0, scalar2=inc[:, r:r + 1], op0=ALU.add, op1=ALU.mult),
              ["sD", "inc"], ["fr0"])
            V(lambda e: e.tensor_scalar(out=fr, in0=fr, scalar1=1.0, scalar2=None, op0=ALU.add), ["fr0"], ["fr"])
            V(lambda e: e.tensor_tensor(out=st.rearrange("p g (r c) -> p (g r) c", r=8), in0=st.rearrange("p g (r c) -> p (g r) c", r=8),
                                        in1=fr.unsqueeze(2).to_broadcast([128, 64, 64]), op=ALU.mult), ["stall", "fr"], ["stall"])
            V(lambda e, r=r: e.scalar_tensor_tensor(out=st.rearrange("p g c -> p (g c)"), in0=sS, scalar=inc[:, r:r + 1],
                                                    in1=st.rearrange("p g c -> p (g c)"), op0=ALU.mult, op1=ALU.add),
              ["stall", "sS", "inc", AO], ["stall"])
        V(lambda e: e.memset(B.stat[:, 5:6], 0.0), ["stall"], [("st", g) for g in range(8)])

    xhalo = B.arena.bitcast(F32)[:, 0:D].rearrange("p (o c) -> p o c", o=1)
    hThf = B.arena[:, 4096:4096 + KT * 128].rearrange("p (k t) -> p k t", k=KT)
    if "ag1_out" in io:
        sel = B.load_cols("sel", io["sel"], 8)
        xtmp = B.arena.bitcast(F32)[:, 4096:4096 + D]
        V(lambda e: e.memset(xhalo[:, 0, :], 0.0), [AO], [("xhalo", 0)] + (["sS"] if full else []))
        for r in range(8):
            I("sp", lambda e, r=r: e.dma_start(out=xtmp, in_=io["ag1_out"][r * 128:(r + 1) * 128, :]), r=[AO, "ag1_out"],
              w=["xtmp"] + (["sS"] if full else []), dma=True)
            V(lambda e, r=r: e.scalar_tensor_tensor(out=xhalo[:, 0, :], in0=xtmp, scalar=sel[:, r:r + 1], in1=xhalo[:, 0, :],
                                                    op0=ALU.mult, op1=ALU.add), ["xtmp", "sel", ("xhalo", 0), AO], [("xhalo", 0)])
    else:
        I("sp", lambda e: e.dma_start(out=xhalo[:, 0, :], in_=xh_in), r=[AO], w=[("xhalo", 0)] + (["sS"] if full else []), dma=True)
    B.norm_T(gm, "gm1", ntt=1, src=xhalo, srckeys=[("xhalo", 0)], dst=hThf, dstkey="hThf", xr=[AO])
    V(lambda e: e.tensor_copy(out=hTh, in_=hThf[:, :, 124:128]), [("hThf", 0), AO], ["hTh"])

    def in_tile(wv, wk, j, b, mt, halo_bank=None, k0=0, nkc=KT):
        def mm(e):
            r = None
            for kk in range(nkc):
                k = k0 + kk
                r = e.matmul(B.ps[b], lhsT=wv[:, kk, j * 128:(j + 1) * 128], rhs=B.hT[:, k, :], start=(k == 0), stop=(k == KT - 1))
            return r
        I("pe", mm, r=[wk] + [("hT", tt) for tt in range(4)], w=[("ps", b)])
        if halo_bank is not None and nkc == KT:
            halo_mm([(wv, wk, 0, KT)], j, halo_bank)

    def halo_mm(chunks, j, halo_bank):
        def mmh(e):
            r = None
            for (wv, wk, k0, nkc) in chunks:
                for kk in range(nkc):
                    k = k0 + kk
                    r = e.matmul(B.ps[halo_bank][:, 8 * j:8 * j + 4], lhsT=wv[:, kk, j * 128:(j + 1) * 128], rhs=hTh[:, k, :],
                                 start=(k == 0), stop=(k == KT - 1))
            return r
        I("pe", mmh, r=[c[1] for c in chunks] + ["hTh"], w=[("ps", halo_bank)])

    def conv_tile(b, ct, mt, out_ap, outkeys, halo_bank, extra_r=(), hj=0):
        if mt == 0:
            V(lambda e: e.tensor_scalar(out=carry[:, ct, :], in0=B.ps[halo_bank][:, 8 * hj + 1:8 * hj + 4], scalar1=hv[:, 0:1], scalar2=None, op0=ALU.mult),
              [("ps", halo_bank), "hvb"], [("carry", ct)])
        A(lambda e: e.activation(out=ubuf[:, 0:3], in_=carry[:, ct, :], func=AF.Copy), [("carry", ct)], ["ubuf_h"])
        A(lambda e: e.activation(out=ubuf[:, 3:515], in_=B.ps[b], func=AF.Copy), [("ps", b)], ["ubuf_m"])
        V(lambda e: e.tensor_copy(out=carry[:, ct, :], in_=ubuf[:, 512:515]), ["ubuf_m"], [("carry", ct)])
        a0, a1 = acc
        V(lambda e: e.tensor_scalar(out=a0, in0=ubuf[:, 0:512], scalar1=cw[:, ct * 4:ct * 4 + 1], scalar2=None, op0=ALU.mult),
          ["ubuf_h", "ubuf_m", "cw"], ["acc0"])
        V(lambda e: e.scalar_tensor_tensor(out=a1, in0=ubuf[:, 1:513], scalar=cw[:, ct * 4 + 1:ct * 4 + 2], in1=a0, op0=ALU.mult, op1=ALU.add),
          ["ubuf_h", "ubuf_m", "cw", "acc0"], ["acc1"])
        V(lambda e: e.scalar_tensor_tensor(out=a0, in0=ubuf[:, 2:514], scalar=cw[:, ct * 4 + 2:ct * 4 + 3], in1=a1, op0=ALU.mult, op1=ALU.add),
          ["ubuf_h", "ubuf_m", "cw", "acc1"], ["acc0"])
        V(lambda e: e.scalar_tensor_tensor(out=a1, in0=ubuf[:, 3:515], scalar=cw[:, ct * 4 + 3:ct * 4 + 4], in1=a0, op0=ALU.mult, op1=ALU.add),
          ["ubuf_m", "cw", "acc0"], ["acc1"])
        A(lambda e: e.activation(out=out_ap, in_=a1, func=AF.Silu, bias=cbias[:, ct:ct + 1], scale=1.0),
          ["acc1", "cbias"] + list(extra_r), list(outkeys))

    outs = []
    for mt in range(NMT):
        B.load_rows(x_in, mt, rkey=x1key)
        B.norm_T(gm, "gm1")
        B.arena_barrier()
        wv, wk = B.wload(Win, 0, KT, 10240, 64)
        for tt in range(4):
            b = B.psbank()

            def mm(e, b=b, tt=tt, wv=wv):
                r = None
                for k in range(KT):
                    r = e.matmul(B.ps[b][:, 0:64], lhsT=B.hT[:, k, tt * 128:(tt + 1) * 128], rhs=wv[:, k, :], start=(k == 0), stop=(k == KT - 1))
                return r
            I("pe", mm, r=[wk, ("hT", tt)], w=[("ps", b)])
            V(lambda e, b=b, tt=tt: e.tensor_tensor(out=dtraw[:, tt, :], in0=B.ps[b][:, 0:64], in1=dtb, op=ALU.add), [("ps", b), "dtb"], [("dtraw", tt)])
        dk = [("dtraw", tt) for tt in range(4)]
        V(lambda e: e.scalar_tensor_tensor(out=dtmp, in0=dtraw, scalar=-1.0, in1=dtraw, op0=ALU.mult, op1=ALU.min), dk, ["dtmp"])
        A(lambda e: e.activation(out=dtmp, in_=dtmp, func=AF.Exp), ["dtmp"], ["dtmp"])
        A(lambda e: e.activation(out=dtmp, in_=dtmp, func=AF.Ln, bias=B.oneb, scale=1.0), ["dtmp", "oneb"], ["dtmp"])
        V(lambda e: e.scalar_tensor_tensor(out=dtv, in0=dtraw, scalar=0.0, in1=dtmp, op0=ALU.max, op1=ALU.add), dk + ["dtmp"], ["dtv"])
        V(lambda e: e.tensor_tensor(out=adt, in0=dtv, in1=a_t.unsqueeze(1).to_broadcast([128, 4, 64]), op=ALU.mult), ["dtv", "a_t"], ["adt"])
        for c in range(2):
            t0, t1 = 2 * c, 2 * c + 1
            b0, b1, b2 = B.psbank(), B.psbank(), B.psbank()
            I("pe", lambda e, b0=b0, t0=t0: e.matmul(B.ps[b0][:, 0:64], lhsT=trif, rhs=adt[:, t0, :], start=True, stop=True),
              r=["trif", "adt"], w=[("ps", b0)])

            def mm1(e, b1=b1, t0=t0, t1=t1):
                e.matmul(B.ps[b1][:, 0:64], lhsT=onesf[:, 0:128], rhs=adt[:, t0, :], start=True, stop=False)
                return e.matmul(B.ps[b1][:, 0:64], lhsT=trif, rhs=adt[:, t1, :], start=False, stop=True)
            I("pe", mm1, r=["trif", "onesf", "adt"], w=[("ps", b1)])

            def mm2(e, b2=b2, t0=t0, t1=t1):
                e.matmul(B.ps[b2][:, 0:64], lhsT=onesf[:, 0:128], rhs=adt[:, t0, :], start=True, stop=False)
                return e.matmul(B.ps[b2][:, 0:64], lhsT=onesf[:, 0:128], rhs=adt[:, t1, :], start=False, stop=True)
            I("pe", mm2, r=["onesf", "adt"], w=[("ps", b2)])
            A(lambda e, b0=b0, t0=t0: e.activation(out=acum[:, t0, :], in_=B.ps[b0][:, 0:64], func=AF.Copy), [("ps", b0)], [("acum", t0)])
            A(lambda e, b1=b1, t1=t1: e.activation(out=acum[:, t1, :], in_=B.ps[b1][:, 0:64], func=AF.Copy), [("ps", b1)], [("acum", t1)])
            A(lambda e, b2=b2, c=c: e.activation(out=tot[:, c, :], in_=B.ps[b2][:, 0:64], func=AF.Copy), [("ps", b2)], [("tot", c)])
            A(lambda e, b2=b2, c=c: e.activation(out=dec[:, c, :], in_=B.ps[b2][:, 0:64], func=AF.Exp), [("ps", b2)], [("dec", c)])
            V(lambda e, c=c: e.tensor_tensor(out=segtot, in0=segtot, in1=tot[:, c, :], op=ALU.add), [("tot", c), "segtot"], ["segtot"])
            for t in (t0, t1):
                V(lambda e, t=t, c=c: e.tensor_tensor(out=te[:, t, :], in0=tot[:, c, :], in1=acum[:, t, :], op=ALU.subtract),
                  [("tot", c), ("acum", t)], [("te0", t)])
                A(lambda e, t=t: e.activation(out=te[:, t, :], in_=te[:, t, :], func=AF.Exp), [("te0", t)], [("te", t)])

        for g in range(8):
            hb = 5 if mt == 0 else None
            if full:
                for c2 in range(2):
                    bz = [B.psbank(), B.psbank()]
                    for hk_ in range(2):
                        wv, wk = B.wload(Win, hk_ * 8 * 128, 8, g * 512 + c2 * 256, 256)
                        for j in range(2):
                            in_tile(wv, wk, j, bz[j], mt, None, hk_ * 8, 8)
                    for j in range(2):
                        A(lambda e, b=bz[j], jj=c2 * 2 + j: e.activation(out=zs[:, jj, :], in_=B.ps[b], func=AF.Silu),
                          [("ps", bz[j]), AO], [("zs", c2 * 2 + j)])
            for c2 in range(2):
                bx = [B.psbank(), B.psbank()]
                chs = []
                for hk_ in range(2):
                    wv, wk = B.wload(Win, hk_ * 8 * 128, 8, 4096 + g * 512 + c2 * 256, 256)
                    chs.append((wv, wk, hk_ * 8, 8))
                    for j in range(2):
                        in_tile(wv, wk, j, bx[j], mt, None, hk_ * 8, 8)
                if hb is not None:
                    for j in range(2):
                        halo_mm(chs, j, hb)
                for j in range(2):
                    jj = c2 * 2 + j
                    conv_tile(bx[j], g * 4 + jj, mt, xT_g[:, jj, :], [("xT", jj)], hb, extra_r=[AO], hj=j)
            wv, wk = B.wload(Win, 0, KT, 8192 + g * 128, 128)
            b = B.psbank()
            in_tile(wv, wk, 0, b, mt, hb)
            conv_tile(b, 32 + g, mt, BT, ["BT"], hb)
            if full:
                wv, wk = B.wload(Win, 0, KT, 9216 + g * 128, 128)
                b = B.psbank()
                in_tile(wv, wk, 0, b, mt, hb)
                conv_tile(b, 40 + g, mt, CT, ["CT"], hb)
            for tt in range(4):
                pt, ptk = B.ptbank()

                def tr(e, pt=pt, tt=tt):
                    r = None
                    for jj in range(4):
                        r = e.transpose(out=pt[:, jj * 128:(jj + 1) * 128], in_=xT_g[:, jj, tt * 128:(tt + 1) * 128], identity=B.ident)
                    return r
                I("pe", tr, r=[("xT", jj) for jj in range(4)] + ["ident", AO], w=[ptk])
                V(lambda e, pt=pt, tt=tt, g=g: e.tensor_tensor(
                    out=xdt[:, tt, :].rearrange("p (r d) -> p r d", r=8), in0=pt[:, 0:512].rearrange("p (r d) -> p r d", r=8),
                    in1=dtv[:, tt, g * 8:(g + 1) * 8].unsqueeze(2).to_broadcast([128, 8, 64]), op=ALU.mult),
                  [ptk, "dtv"], [("xdt", tt)])
            pt, ptk = B.ptbank()

            def trb(e, pt=pt):
                r = None
                for tt in range(4):
                    r = e.transpose(out=pt[:, tt * 128:(tt + 1) * 128], in_=BT[:, tt * 128:(tt + 1) * 128], identity=B.ident)
                return r
            I("pe", trb, r=["BT", "ident"], w=[ptk])
            A(lambda e, pt=pt: e.activation(out=Btm.rearrange("p t n -> p (t n)"), in_=pt[:, 0:512], func=AF.Copy), [ptk], ["Btm"])

            for c in range(2):
                t0, t1 = 2 * c, 2 * c + 1
                cs = slice(c * 256, (c + 1) * 256)
                if full:
                    A(lambda e, g=g: e.activation(out=stbf, in_=st[:, g, :], func=AF.Copy), [("st", g)], ["stbf"])
                    for j, tj in enumerate((t0, t1)):
                        b = B.psbank()
                        I("pe", lambda e, b=b, tj=tj, cs=cs: e.matmul(B.ps[b][:, 0:256], lhsT=BT[:, tj * 128:(tj + 1) * 128], rhs=CT[:, cs],
                                                                      start=True, stop=True), r=["BT", "CT"], w=[("ps", b)])
                        trj, trk = (tri0, "tri0") if j == 0 else (tri1, "tri1")
                        V(lambda e, b=b, j=j, trj=trj: e.tensor_tensor(out=CBm[j], in0=B.ps[b][:, 0:256], in1=trj, op=ALU.mult),
                          [("ps", b), trk], [("CBm", j)])
                    ybanks = [3, 4]
                    for r in range(8):
                        h = g * 8 + r
                        jj, half = r // 2, r % 2
                        V(lambda e, h=h, t0=t0: e.tensor_copy(out=adtrep[0], in_=adt[:, t0, h:h + 1].to_broadcast([128, 128])), ["adt"], [("adtrep", 0)])
                        V(lambda e, h=h, t1=t1: e.tensor_copy(out=adtrep[1], in_=adt[:, t1, h:h + 1].to_broadcast([128, 128])), ["adt"], [("adtrep", 1)])
                        bb = B.psbank()

                        def mmb(e, bb=bb):
                            e.matmul(B.ps[bb][:, 0:256], lhsT=adtrep[0], rhs=tri0, start=True, stop=False)
                            return e.matmul(B.ps[bb][:, 0:256], lhsT=adtrep[1], rhs=tri1, start=False, stop=True)
                        I("pe", mmb, r=[("adtrep", 0), ("adtrep", 1), "tri0", "tri1"], w=[("ps", bb)])
                        for j, tj in enumerate((t0, t1)):
                            V(lambda e, bb=bb, j=j, tj=tj, h=h: e.tensor_scalar(out=darg[j], in0=B.ps[bb][:, 0:256], scalar1=acum[:, tj, h:h + 1],
                                                                                scalar2=0.0, op0=ALU.subtract, op1=ALU.min),
                              [("ps", bb), ("acum", tj)], [("darg", j)])
                            A(lambda e, j=j: e.activation(out=darg[j], in_=darg[j], func=AF.Exp), [("darg", j)], [("darg", j)])
                            I("pool", lambda e, j=j: e.tensor_tensor(out=Mt[j], in0=darg[j], in1=CBm[j], op=ALU.mult), [("darg", j), ("CBm", j)], [("Mt", j)])
                        A(lambda e, bb=bb: e.activation(out=eoff, in_=B.ps[bb][:, 0:256], func=AF.Exp), [("ps", bb)], ["eoff"])
                        I("pool", lambda e, cs=cs: e.tensor_tensor(out=roff, in0=eoff, in1=CT[:, cs], op=ALU.mult), ["eoff", "CT"], ["roff"])
                        yb = ybanks[jj // 2]
                        yo = B.ps[yb][half * 64:(half + 1) * 64, (jj % 2) * 256:(jj % 2) * 256 + 256]

                        def mmy(e, yo=yo, r=r, t0=t0, t1=t1):
                            e.matmul(yo, lhsT=xdt[:, t0, r * 64:(r + 1) * 64], rhs=Mt[0], start=True, stop=False)
                            e.matmul(yo, lhsT=xdt[:, t1, r * 64:(r + 1) * 64], rhs=Mt[1], start=False, stop=False)
                            return e.matmul(yo, lhsT=stbf[:, r * 64:(r + 1) * 64], rhs=roff, start=False, stop=True)
                        I("pe", mmy, r=[("xdt", t0), ("xdt", t1), ("Mt", 0), ("Mt", 1), "stbf", "roff"], w=[("ps", yb)])
                    for jj in range(4):
                        yb = ybanks[jj // 2]
                        V(lambda e, jj=jj, yb=yb, cs=cs, g=g: e.scalar_tensor_tensor(
                            out=yg[:, jj, :], in0=xT_g[:, jj, cs], scalar=dcol[:, g * 4 + jj:g * 4 + jj + 1],
                            in1=B.ps[yb][:, (jj % 2) * 256:(jj % 2) * 256 + 256], op0=ALU.mult, op1=ALU.add),
                          [("ps", yb), ("xT", jj), "dcol", AO], [("yg", jj)])
                        V(lambda e, jj=jj, cs=cs: e.tensor_tensor(out=yg[:, jj, :], in0=yg[:, jj, :], in1=zs[:, jj, cs], op=ALU.mult),
                          [("yg", jj), ("zs", jj), AO], [("yg", jj)])
                        A(lambda e, jj=jj: e.activation(out=sqn[:, jj, :], in_=yg[:, jj, :], func=AF.Square), [("yg", jj), AO], [("sqn", jj)])
                    bs = B.psbank()

                    def mms(e, bs=bs):
                        r = None
                        for jj in range(4):
                            r = e.matmul(B.ps[bs][:, 0:256], lhsT=B.ones, rhs=sqn[:, jj, :], start=(jj == 0), stop=(jj == 3))
                        return r
                    I("pe", mms, r=["ones"] + [("sqn", jj) for jj in range(4)], w=[("ps", bs)])
                    A(lambda e, bs=bs: e.activation(out=rstd, in_=B.ps[bs][:, 0:256], func=AF.Sqrt, bias=B.epsb, scale=1.0 / 512),
                      [("ps", bs), "epsb"], ["rstd0"])
                    V(lambda e: e.reciprocal(out=rstd, in_=rstd), ["rstd0"], ["rstd"])
                    for jj in range(4):
                        V(lambda e, jj=jj, cs=cs, g=g: e.scalar_tensor_tensor(
                            out=ynT[:, g * 4 + jj, cs], in0=yg[:, jj, :], scalar=ngcol[:, g * 4 + jj:g * 4 + jj + 1], in1=rstd,
                            op0=ALU.mult, op1=ALU.mult), [("yg", jj), "rstd", "ngcol", AO], [("ynT", g * 4 + jj)])
                for i, t in enumerate((t0, t1)):
                    V(lambda e, i=i, t=t, g=g: e.tensor_tensor(
                        out=xdte[:, i, :].rearrange("p (r d) -> p r d", r=8), in0=xdt[:, t, :].rearrange("p (r d) -> p r d", r=8),
                        in1=te[:, t, g * 8:(g + 1) * 8].unsqueeze(2).to_broadcast([128, 8, 64]), op=ALU.mult),
                      [("xdt", t), ("te", t)], [("xdte", i)])
                bn = B.psbank()

                def mmn(e, bn=bn, t0=t0, t1=t1):
                    e.matmul(B.ps[bn], lhsT=Btm[:, t0, :], rhs=xdte[:, 0, :], start=True, stop=False)
                    return e.matmul(B.ps[bn], lhsT=Btm[:, t1, :], rhs=xdte[:, 1, :], start=False, stop=True)
                I("pe", mmn, r=["Btm", ("xdte", 0), ("xdte", 1)], w=[("ps", bn)])
                V(lambda e, g=g, c=c: e.tensor_tensor(
                    out=st[:, g, :].rearrange("p (r d) -> p r d", r=8), in0=st[:, g, :].rearrange("p (r d) -> p r d", r=8),
                    in1=dec[:, c, g * 8:(g + 1) * 8].unsqueeze(2).to_broadcast([128, 8, 64]), op=ALU.mult),
                  [("st", g), ("dec", c)], [("st", g)])
                V(lambda e, g=g, bn=bn: e.tensor_tensor(out=st[:, g, :], in0=st[:, g, :], in1=B.ps[bn], op=ALU.add),
                  [("st", g), ("ps", bn)], [("st", g)])
        if full:
            yk = [("ynT", k) for k in range(32)] + [AO]
            B.nrot = 6
            B.proj_out(ynT, yk, 32, io["wout"])
            B.ffn(gf, "gf1", io["wg"], io["wu"], io["wd"])
            outs += B.store_rows(io["y"], mt)
            B.nrot = 3
    if not full:
        A(lambda e: e.activation(out=segtot, in_=segtot, func=AF.Exp), ["segtot"], ["segdec"])
        if "ag2_in" in io:
            I("sp", lambda e: e.dma_start(out=io["ag2_in"][:, 0:4096], in_=st.rearrange("p g c -> p (g c)")),
              r=[("st", g) for g in range(8)], w=["ag2_in"], dma=True)
            I("sp", lambda e: e.dma_start(out=io["ag2_in"][:, 4096:4160], in_=segtot), r=["segdec"], w=["ag2_in"], dma=True)
            I("pool", lambda e: e.collective_compute("AllGather", ALU.bypass, replica_groups=[list(range(NCORES))],
                                                     ins=[io["ag2_in"].opt()], outs=[io["ag2_out"].opt()]),
              r=["ag2_in"], w=["ag2_out"], cc=True)
        else:
            outs.append(I("sp", lambda e: e.dma_start(out=io["stS_out"], in_=st.rearrange("p g c -> p (g c)")),
                          r=[("st", g) for g in range(8)], dma=True))
            outs.append(I("sp", lambda e: e.dma_start(out=io["stD_out"], in_=segtot), r=["segdec"], dma=True))
    return outs


def _col(v, n):
    return np.ascontiguousarray(np.asarray(v, np.float32).reshape(n, 128).T)


def _tile(v):
    v = np.asarray(v)
    return np.ascontiguousarray(np.broadcast_to(v[None, :], (128, v.shape[0])))


def build_A(cache=False, nstage=2):
    B = Builder(nstage=nstage)
    if cache:
        B.enable_weight_cache((D * (QKVW + D + 3 * DFF)) // 128 + 4096, names=("wg", "wu", "wd"))
    io = {}
    for name, shape, dt in (("xs", [TOK, D], F32), ("xh", [128, D], F32), ("pos", [128, TOK // 128 + 1], I32), ("hv", [128, 1], F32),
                            ("gm0", [128, KT], F32), ("gf0", [128, KT], F32), ("gq", [128, 64], F32), ("gk", [128, 64], F32),
                            ("snk", [128, 32], F32), ("wqkv", [D, QKVW], F32), ("wo", [D, D], F32),
                            ("wg", [D, DFF], F32), ("wu", [D, DFF], F32), ("wd", [DFF, D], F32)):
        io[name] = B.dram(name, shape, dt, "ExternalInput")
    io["x1"] = B.dram("x1", [TOK, D], F32, "ExternalOutput")
    outs = phase_A(B, io)
    B.S.emit(final_wait_ops=outs)
    return B.nc


def inputs_A(inp):
    x = np.asarray(inp["x"], np.float32)
    pos = np.asarray(inp["positions"], np.int32)
    maps = []
    for c in range(NCORES):
        b, q = c // 4, c % 4
        s0 = q * TOK
        if q == 0:
            xh = np.zeros((128, D), np.float32)
            ph = pos[b, 0:128]
        else:
            xh = x[b, s0 - 128:s0]
            ph = pos[b, s0 - 128:s0]
        pp = np.concatenate([ph, pos[b, s0:s0 + TOK]]).reshape(TOK // 128 + 1, 128).T
        maps.append({
            "xs": np.ascontiguousarray(x[b, s0:s0 + TOK]), "xh": np.ascontiguousarray(xh),
            "pos": np.ascontiguousarray(pp.astype(np.int32)),
            "hv": np.full((128, 1), 0.0 if q == 0 else 1.0, np.float32),
            "gm0": _col(inp["mixer_norm"][0], KT), "gf0": _col(inp["ffn_norm"][0], KT),
            "gq": _tile(np.asarray(inp["attn_q_norm"], np.float32)[0]), "gk": _tile(np.asarray(inp["attn_k_norm"], np.float32)[0]),
            "snk": _tile(np.asarray(inp["attn_sinks"], np.float32)[0]),
            "wqkv": np.asarray(inp["attn_w_qkv"], np.float32)[0], "wo": np.asarray(inp["attn_w_o"], np.float32)[0],
            "wg": np.asarray(inp["ffn_w_gate"], np.float32)[0], "wu": np.asarray(inp["ffn_w_up"], np.float32)[0],
            "wd": np.asarray(inp["ffn_w_down"], np.float32)[0],
        })
    return maps


def _b_io(B, full):
    io = {}
    ins = [("x1", [TOK, D], F32), ("x1h", [128, D], F32), ("hv", [128, 1], F32), ("gm1", [128, KT], F32),
           ("cw", [128, 48 * 4], F32), ("cb", [128, 48], F32), ("dtb", [128, 64], F32), ("alog", [128, 64], F32),
           ("win", [D, INW], F32)]
    if full:
        ins += [("dcol", [128, 32], F32), ("ngcol", [128, 32], F32), ("gf1", [128, KT], F32), ("inc", [128, 4], F32),
                ("stS", [4, 128, 4096], F32), ("stD", [4, 128, 64], F32), ("wout", [DIN, D], F32),
                ("wg", [D, DFF], F32), ("wu", [D, DFF], F32), ("wd", [DFF, D], F32)]
    for name, shape, dt in ins:
        io[name] = B.dram(name, shape, dt, "ExternalInput")
    if full:
        io["y"] = B.dram("y", [TOK, D], F32, "ExternalOutput")
    else:
        io["stS_out"] = B.dram("stS_out", [128, 4096], F32, "ExternalOutput")
        io["stD_out"] = B.dram("stD_out", [128, 64], F32, "ExternalOutput")
    return io


def build_B(full):
    B = Builder(nstage=1)
    io = _b_io(B, full)
    outs = phase_B(B, io, full)
    B.S.emit(final_wait_ops=outs)
    return B.nc


def inputs_B(inp, x1, full, states=None):
    x1 = np.asarray(x1, np.float32)
    cw = np.asarray(inp["ssm_conv_w"], np.float32)[0]
    cwl = np.ascontiguousarray(cw.T.reshape(48, 128, 4).transpose(1, 0, 2).reshape(128, 192))
    maps = []
    for c in range(NCORES):
        b, q = c // 4, c % 4
        s0 = q * TOK
        xh = np.zeros((128, D), np.float32) if q == 0 else x1[b, s0 - 128:s0]
        m = {
            "x1": np.ascontiguousarray(x1[b, s0:s0 + TOK]), "x1h": np.ascontiguousarray(xh),
            "hv": np.full((128, 1), 0.0 if q == 0 else 1.0, np.float32),
            "gm1": _col(inp["mixer_norm"][1], KT), "cw": cwl, "cb": _col(np.asarray(inp["ssm_conv_b"], np.float32)[0], 48),
            "dtb": _tile(np.asarray(inp["ssm_dt_bias"], np.float32)[0]), "alog": _tile(np.asarray(inp["ssm_a_log"], np.float32)[0]),
            "win": np.asarray(inp["ssm_w_in"], np.float32)[0],
        }
        if full:
            m.update({
                "dcol": _col(np.repeat(np.asarray(inp["ssm_d"], np.float32)[0], 64), 32),
                "ngcol": _col(np.asarray(inp["ssm_norm"], np.float32)[0], 32),
                "gf1": _col(inp["ffn_norm"][1], KT),
                "inc": np.ascontiguousarray(np.broadcast_to((np.arange(4) < q).astype(np.float32)[None, :], (128, 4))),
                "stS": np.ascontiguousarray(np.stack([states[b * 4 + r][0] for r in range(4)])),
                "stD": np.ascontiguousarray(np.stack([states[b * 4 + r][1] for r in range(4)])),
                "wout": np.asarray(inp["ssm_w_out"], np.float32)[0],
                "wg": np.asarray(inp["ffn_w_gate"], np.float32)[1], "wu": np.asarray(inp["ffn_w_up"], np.float32)[1],
                "wd": np.asarray(inp["ffn_w_down"], np.float32)[1],
            })
        maps.append(m)
    return maps


def build_fused():
    B = Builder(nstage=2)
    io = {}
    for name, shape, dt in (("xs", [TOK, D], F32), ("xh", [128, D], F32), ("pos", [128, TOK // 128 + 1], I32), ("hv", [128, 1], F32),
                            ("gm0", [128, KT], F32), ("gf0", [128, KT], F32), ("gq", [128, 64], F32), ("gk", [128, 64], F32),
                            ("snk", [128, 32], F32), ("wqkv", [D, QKVW], F32), ("wo", [D, D], F32),
                            ("wg0", [D, DFF], F32), ("wu0", [D, DFF], F32), ("wd0", [DFF, D], F32),
                            ("gm1", [128, KT], F32), ("cw", [128, 48 * 4], F32), ("cb", [128, 48], F32), ("dtb", [128, 64], F32),
                            ("alog", [128, 64], F32), ("win", [D, INW], F32), ("dcol", [128, 32], F32), ("ngcol", [128, 32], F32),
                            ("gf1", [128, KT], F32), ("inc", [128, 8], F32), ("sel", [128, 8], F32), ("wout", [DIN, D], F32),
                            ("wg1", [D, DFF], F32), ("wu1", [D, DFF], F32), ("wd1", [DFF, D], F32)):
        io[name] = B.dram(name, shape, dt, "ExternalInput")
    io["y"] = B.dram("y", [TOK, D], F32, "ExternalOutput")
    io["x1"] = B.dram("x1_d", [TOK, D], F32)
    io["ag1_in"] = B.dram("ag1_in", [128, D], F32)
    io["ag1_out"] = B.dram("ag1_out", [NCORES * 128, D], F32)
    io["ag2_in"] = B.dram("ag2_in", [128, 4160], F32)
    io["ag2_out"] = B.dram("ag2_out", [NCORES * 128, 4160], F32)
    B.enable_phase_scratch()
    ioA = dict(io); ioA.update(wg=io["wg0"], wu=io["wu0"], wd=io["wd0"])
    B.begin_phase("A")
    phase_A(B, ioA)
    ioB = dict(io); ioB.update(wg=io["wg1"], wu=io["wu1"], wd=io["wd1"])
    B.begin_phase("B1")
    phase_B(B, ioB, False)
    B.begin_phase("B2")
    outs = phase_B(B, ioB, True)
    B.S.emit(final_wait_ops=outs)
    return B.nc


def inputs_fused(inp):
    mA = inputs_A(inp)
    zero_x1 = np.zeros((2, SEQ, D), np.float32)
    maps = []
    cw = np.asarray(inp["ssm_conv_w"], np.float32)[0]
    cwl = np.ascontiguousarray(cw.T.reshape(48, 128, 4).transpose(1, 0, 2).reshape(128, 192))
    for c in range(NCORES):
        b, q = c // 4, c % 4
        m = dict(mA[c])
        m["wg0"], m["wu0"], m["wd0"] = m.pop("wg"), m.pop("wu"), m.pop("wd")
        inc = np.array([1.0 if (r // 4 == b and r % 4 < q) else 0.0 for r in range(8)], np.float32)
        sel = np.array([1.0 if (q > 0 and r == c - 1) else 0.0 for r in range(8)], np.float32)
        m.update({
            "gm1": _col(inp["mixer_norm"][1], KT), "cw": cwl, "cb": _col(np.asarray(inp["ssm_conv_b"], np.float32)[0], 48),
            "dtb": _tile(np.asarray(inp["ssm_dt_bias"], np.float32)[0]), "alog": _tile(np.asarray(inp["ssm_a_log"], np.float32)[0]),
            "win": np.asarray(inp["ssm_w_in"], np.float32)[0],
            "dcol": _col(np.repeat(np.asarray(inp["ssm_d"], np.float32)[0], 64), 32),
            "ngcol": _col(np.asarray(inp["ssm_norm"], np.float32)[0], 32),
            "gf1": _col(inp["ffn_norm"][1], KT), "inc": _tile(inc), "sel": _tile(sel),
            "wout": np.asarray(inp["ssm_w_out"], np.float32)[0],
            "wg1": np.asarray(inp["ffn_w_gate"], np.float32)[1], "wu1": np.asarray(inp["ffn_w_up"], np.float32)[1],
            "wd1": np.asarray(inp["ffn_w_down"], np.float32)[1],
        })
        maps.append(m)
    return maps


_NC_CACHE = {}


def kernel(**inp):
    if "F" not in _NC_CACHE:
        _NC_CACHE["F"] = build_fused()
    res = run_bass_kernel_spmd(_NC_CACHE["F"], inputs_fused(inp), core_ids=list(range(NCORES)))
    out = np.stack([r["y"] for r in res.results]).reshape(2, SEQ, D)
    return out.astype(np.float32)
```
